# Optimizing a Trainium2 kernel written in Bass

```python
import math
import jax, jax.numpy as jnp
from jax import lax
import numpy as np

D_MODEL = 2048
BATCH = 8
SEQ = 2048
DEPTH = 1

D_MIX = D_MODEL
D_ATTN = D_MIX // 2
D_SSM = D_MIX - D_ATTN
HEAD_DIM = 128
N_HEADS = D_ATTN // HEAD_DIM
N_KV = 2
HEADS_PER_KV = N_HEADS // N_KV
D_KV = N_KV * HEAD_DIM
CMP_LEN = 32
CMP_STRIDE = 16
CMP_HIDDEN = HEAD_DIM
SEL_BLOCK = 64
N_SELECT = 16
WINDOW = 512
WIN_Q_BLOCK = 128
SEL_Q_BLOCK = 32
N_BUCKETS = 32
MAX_DISTANCE = 128
SSM_GROUP = 16
SSM_GROUPS = D_SSM // SSM_GROUP
SSM_STATE = 64
D_FF = 5632
EPS = 1e-6
N_IN = D_ATTN + 6 * D_KV + 3 * N_HEADS + D_SSM
NEG = -1e30

kernel_name = "hymba_nsa_s5_macaron_block"


def rmsnorm(x, g):
    xf = x.astype(jnp.float32)
    y = xf * lax.rsqrt(jnp.mean(xf * xf, axis=-1, keepdims=True) + EPS)
    return (y * g.astype(jnp.float32)).astype(x.dtype)


def swiglu(x, w1, w3, w2):
    return (jax.nn.silu(x @ w1) * (x @ w3)) @ w2


def t5_bucket(dist):
    n = jnp.maximum(dist, 0)
    max_exact = N_BUCKETS // 2
    nf = jnp.maximum(n, 1).astype(jnp.float32)
    large = max_exact + (jnp.log(nf / max_exact) / math.log(MAX_DISTANCE / max_exact)
                         * (N_BUCKETS - max_exact)).astype(jnp.int32)
    large = jnp.minimum(large, N_BUCKETS - 1)
    return jnp.where(n < max_exact, n, large)


def masked_softmax(s, mask):
    s = jnp.where(mask, s, NEG)
    m = jnp.max(s, axis=-1, keepdims=True)
    p = jnp.exp(s - m) * mask
    return p / jnp.maximum(jnp.sum(p, axis=-1, keepdims=True), 1e-30)


def compress_blocks(k, pe, w1, b1, w2):
    B, T, G, dh = k.shape
    nc = (T - CMP_LEN) // CMP_STRIDE + 1
    idx = jnp.arange(nc)[:, None] * CMP_STRIDE + jnp.arange(CMP_LEN)[None, :]
    blk = k[:, idx] + pe[None, None, :, None, :]
    blk = blk.transpose(0, 1, 3, 2, 4).reshape(B, nc, G, CMP_LEN * dh)
    return jax.nn.gelu(blk @ w1 + b1) @ w2


def nsa_mixer(q, kc, vc, ks, vs, kw, vw, gate_logits, rel_bias,
              pe_k, w1_k, b1_k, w2_k, pe_v, w1_v, b1_v, w2_v):
    B, T, _ = q.shape
    G, Hg, dh = N_KV, HEADS_PER_KV, HEAD_DIM
    q = q.reshape(B, T, G, Hg, dh) * (dh ** -0.5)
    kc, vc, ks, vs, kw, vw = [a.reshape(B, T, G, dh) for a in (kc, vc, ks, vs, kw, vw)]
    t = jnp.arange(T)
    table_g = rel_bias.reshape(N_BUCKETS, G, Hg)

    kcb = compress_blocks(kc, pe_k, w1_k, b1_k, w2_k)
    vcb = compress_blocks(vc, pe_v, w1_v, b1_v, w2_v)
    nc = kcb.shape[1]
    c_start = jnp.arange(nc) * CMP_STRIDE
    c_end = c_start + CMP_LEN - 1
    dist_c = t[:, None] - c_end[None, :]
    s_c = jnp.einsum('btghd,bcgd->bghtc', q, kcb).astype(jnp.float32)
    s_c = s_c + table_g[t5_bucket(dist_c)].transpose(2, 3, 0, 1).astype(jnp.float32)
    p_cmp = masked_softmax(s_c, dist_c >= 0)
    o_cmp = jnp.einsum('bghtc,bcgd->btghd', p_cmp.astype(vcb.dtype), vcb)

    ns = T // SEL_BLOCK
    n_sel = min(N_SELECT, ns)
    j_start = jnp.arange(ns) * SEL_BLOCK
    overlap = jnp.clip(jnp.minimum(c_start[:, None] + CMP_LEN, j_start[None, :] + SEL_BLOCK)
                       - jnp.maximum(c_start[:, None], j_start[None, :]), 0, None)
    overlap = overlap.astype(jnp.float32) / CMP_LEN
    imp = jnp.einsum('bghtc,cj->bgtj', p_cmp, overlap)
    cur = t // SEL_BLOCK
    jj = jnp.arange(ns)
    forced = (jj[None, :] == 0) | (jj[None, :] == cur[:, None]) | (jj[None, :] == cur[:, None] - 1)
    causal_blk = j_start[None, :] <= t[:, None]
    imp = jnp.where(forced, 1e6, jnp.where(causal_blk, imp, -1e9))
    _, sel_idx = lax.top_k(imp, n_sel)

    ks_b = ks.reshape(B, ns, SEL_BLOCK, G, dh).transpose(0, 3, 1, 2, 4)
    vs_b = vs.reshape(B, ns, SEL_BLOCK, G, dh).transpose(0, 3, 1, 2, 4)
    nq = T // SEL_Q_BLOCK
    q_ch = q.reshape(B, nq, SEL_Q_BLOCK, G, Hg, dh).transpose(1, 0, 2, 3, 4, 5)
    idx_ch = sel_idx.reshape(B, G, nq, SEL_Q_BLOCK, n_sel).transpose(2, 0, 1, 3, 4)
    t_ch = t.reshape(nq, SEL_Q_BLOCK)
    bi = jnp.arange(B)[:, None, None, None]
    gi = jnp.arange(G)[None, :, None, None]
    g_b = jnp.arange(G)[None, :, None, None, None]
    s_off = jnp.arange(SEL_BLOCK)

    def sel_block(args):
        qc, ic, tc = args
        kg = ks_b[bi, gi, ic]
        vg = vs_b[bi, gi, ic]
        s = jnp.einsum('bqghd,bgqnsd->bghqns', qc, kg).astype(jnp.float32)
        kpos = ic[..., None] * SEL_BLOCK + s_off
        dist = tc[None, None, :, None, None] - kpos
        bias = table_g[t5_bucket(dist), g_b].transpose(0, 1, 5, 2, 3, 4)
        s = (s + bias.astype(jnp.float32)).reshape(B, G, Hg, SEL_Q_BLOCK, n_sel * SEL_BLOCK)
        mask = (dist >= 0).reshape(B, G, 1, SEL_Q_BLOCK, n_sel * SEL_BLOCK)
        p = masked_softmax(s, mask).reshape(B, G, Hg, SEL_Q_BLOCK, n_sel, SEL_BLOCK)
        return jnp.einsum('bghqns,bgqnsd->bqghd', p.astype(vg.dtype), vg)

    o_sel = lax.map(sel_block, (q_ch, idx_ch, t_ch))
    o_sel = o_sel.transpose(1, 0, 2, 3, 4, 5).reshape(B, T, G, Hg, dh)

    nb = T // WIN_Q_BLOCK
    span = WIN_Q_BLOCK + WINDOW
    kw_pad = jnp.pad(kw, ((0, 0), (WINDOW, 0), (0, 0), (0, 0)))
    vw_pad = jnp.pad(vw, ((0, 0), (WINDOW, 0), (0, 0), (0, 0)))
    kidx = jnp.arange(nb)[:, None] * WIN_Q_BLOCK + jnp.arange(span)[None, :]
    kwb = kw_pad[:, kidx]
    vwb = vw_pad[:, kidx]
    qb = q.reshape(B, nb, WIN_Q_BLOCK, G, Hg, dh)
    s_w = jnp.einsum('bnqghd,bnkgd->bnghqk', qb, kwb).astype(jnp.float32)
    qpos = jnp.arange(nb)[:, None] * WIN_Q_BLOCK + jnp.arange(WIN_Q_BLOCK)[None, :]
    kpos = kidx - WINDOW
    dist_w = qpos[:, :, None] - kpos[:, None, :]
    mask_w = (dist_w >= 0) & (dist_w < WINDOW) & (kpos[:, None, :] >= 0)
    bias_w = table_g[t5_bucket(dist_w)].transpose(0, 3, 4, 1, 2)
    s_w = s_w + bias_w[None].astype(jnp.float32)
    p_w = masked_softmax(s_w, mask_w[:, None, None])
    o_win = jnp.einsum('bnghqk,bnkgd->bnqghd', p_w.astype(vwb.dtype), vwb).reshape(B, T, G, Hg, dh)

    g = jax.nn.sigmoid(gate_logits.astype(jnp.float32)).reshape(B, T, G, Hg, 3).astype(q.dtype)
    o = g[..., 0:1] * o_cmp + g[..., 1:2] * o_sel + g[..., 2:3] * o_win
    return o.reshape(B, T, D_ATTN)


def s5_mixer(u, lam_re, lam_im, log_step, b_re, b_im, c_re, c_im, d_skip, w_glu, b_glu):
    B, T, _ = u.shape
    f32 = jnp.float32
    uf = u.astype(f32).reshape(B, T, SSM_GROUPS, SSM_GROUP)
    step = jnp.exp(log_step.astype(f32))[:, None]
    lre, lim = lam_re.astype(f32), lam_im.astype(f32)
    mag = jnp.exp(lre * step)
    ab_re, ab_im = mag * jnp.cos(lim * step), mag * jnp.sin(lim * step)
    nr, ni = ab_re - 1.0, ab_im
    den = lre * lre + lim * lim
    f_re, f_im = (nr * lre + ni * lim) / den, (ni * lre - nr * lim) / den
    br, bim = b_re.astype(f32), b_im.astype(f32)
    bb_re = f_re[..., None] * br - f_im[..., None] * bim
    bb_im = f_re[..., None] * bim + f_im[..., None] * br
    bu_re = jnp.einsum('gph,btgh->btgp', bb_re, uf)
    bu_im = jnp.einsum('gph,btgh->btgp', bb_im, uf)
    a_re = jnp.broadcast_to(ab_re, (1, T, SSM_GROUPS, SSM_STATE))
    a_im = jnp.broadcast_to(ab_im, (1, T, SSM_GROUPS, SSM_STATE))

    def combine(e1, e2):
        a1r, a1i, b1r, b1i = e1
        a2r, a2i, b2r, b2i = e2
        return (a2r * a1r - a2i * a1i, a2r * a1i + a2i * a1r,
                a2r * b1r - a2i * b1i + b2r, a2r * b1i + a2i * b1r + b2i)

    _, _, xr, xi = lax.associative_scan(combine, (a_re, a_im, bu_re, bu_im), axis=1)
    y = (jnp.einsum('ghp,btgp->btgh', c_re.astype(f32), xr)
         - jnp.einsum('ghp,btgp->btgh', c_im.astype(f32), xi)
         + d_skip.astype(f32).reshape(SSM_GROUPS, SSM_GROUP) * uf)
    y = y.reshape(B, T, D_SSM).astype(u.dtype)
    h = jax.nn.gelu(y)
    return h * jax.nn.sigmoid(h @ w_glu + b_glu)


def setup_inputs(seed: int = 0) -> dict:
    key = jax.random.key(seed)
    ks = iter(jax.random.split(key, 40))
    f32 = jnp.float32

    def nrm(shape, scale):
        return jax.random.normal(next(ks), shape, f32) * scale

    def gain(shape):
        return 1.0 + nrm(shape, 0.02)

    L = DEPTH
    P = SSM_STATE
    lam_im0 = math.pi * jnp.arange(P, dtype=f32)
    return {
        "x": nrm((BATCH, SEQ, D_MODEL), 1.0),
        "ffn1_norm": gain((L, D_MODEL)),
        "ffn1_w1": nrm((L, D_MODEL, D_FF), D_MODEL ** -0.5),
        "ffn1_w3": nrm((L, D_MODEL, D_FF), D_MODEL ** -0.5),
        "ffn1_w2": nrm((L, D_FF, D_MODEL), D_FF ** -0.5),
        "mix_norm": gain((L, D_MODEL)),
        "w_in": nrm((L, D_MODEL, N_IN), D_MODEL ** -0.5),
        "cmp_pe_k": nrm((L, CMP_LEN, HEAD_DIM), 0.1),
        "cmp_w1_k": nrm((L, CMP_LEN * HEAD_DIM, CMP_HIDDEN), (CMP_LEN * HEAD_DIM) ** -0.5),
        "cmp_b1_k": nrm((L, CMP_HIDDEN), 0.01),
        "cmp_w2_k": nrm((L, CMP_HIDDEN, HEAD_DIM), CMP_HIDDEN ** -0.5),
        "cmp_pe_v": nrm((L, CMP_LEN, HEAD_DIM), 0.1),
        "cmp_w1_v": nrm((L, CMP_LEN * HEAD_DIM, CMP_HIDDEN), (CMP_LEN * HEAD_DIM) ** -0.5),
        "cmp_b1_v": nrm((L, CMP_HIDDEN), 0.01),
        "cmp_w2_v": nrm((L, CMP_HIDDEN, HEAD_DIM), CMP_HIDDEN ** -0.5),
        "rel_bias": nrm((N_BUCKETS, N_HEADS), 0.5),
        "ssm_lam_re": -0.5 + nrm((L, SSM_GROUPS, P), 0.01),
        "ssm_lam_im": lam_im0[None, None, :] + nrm((L, SSM_GROUPS, P), 0.01),
        "ssm_log_step": jax.random.uniform(next(ks), (L, SSM_GROUPS), f32,
                                           math.log(1e-3), math.log(1e-1)),
        "ssm_b_re": nrm((L, SSM_GROUPS, P, SSM_GROUP), (2.0 * SSM_GROUP) ** -0.5),
        "ssm_b_im": nrm((L, SSM_GROUPS, P, SSM_GROUP), (2.0 * SSM_GROUP) ** -0.5),
        "ssm_c_re": nrm((L, SSM_GROUPS, SSM_GROUP, P), (2.0 * P) ** -0.5),
        "ssm_c_im": nrm((L, SSM_GROUPS, SSM_GROUP, P), (2.0 * P) ** -0.5),
        "ssm_d": nrm((L, D_SSM), 1.0),
        "glu_w": nrm((L, D_SSM, D_SSM), D_SSM ** -0.5),
        "glu_b": nrm((L, D_SSM), 0.01),
        "w_out": nrm((L, D_MIX, D_MODEL), D_MIX ** -0.5),
        "ffn2_norm": gain((L, D_MODEL)),
        "ffn2_w1": nrm((L, D_MODEL, D_FF), D_MODEL ** -0.5),
        "ffn2_w3": nrm((L, D_MODEL, D_FF), D_MODEL ** -0.5),
        "ffn2_w2": nrm((L, D_FF, D_MODEL), D_FF ** -0.5),
        "final_norm": gain((D_MODEL,)),
    }


def reference(x, ffn1_norm, ffn1_w1, ffn1_w3, ffn1_w2, mix_norm, w_in,
              cmp_pe_k, cmp_w1_k, cmp_b1_k, cmp_w2_k, cmp_pe_v, cmp_w1_v, cmp_b1_v, cmp_w2_v,
              rel_bias, ssm_lam_re, ssm_lam_im, ssm_log_step, ssm_b_re, ssm_b_im,
              ssm_c_re, ssm_c_im, ssm_d, glu_w, glu_b, w_out,
              ffn2_norm, ffn2_w1, ffn2_w3, ffn2_w2, final_norm):
    splits = np.cumsum([D_ATTN, D_KV, D_KV, D_KV, D_KV, D_KV, D_KV, 3 * N_HEADS])
    h = x
    for l in range(DEPTH):
        h = h + 0.5 * swiglu(rmsnorm(h, ffn1_norm[l]), ffn1_w1[l], ffn1_w3[l], ffn1_w2[l])
        proj = rmsnorm(h, mix_norm[l]) @ w_in[l]
        q, kc, vc, ksl, vsl, kw, vw, gates, u_ssm = jnp.split(proj, splits, axis=-1)
        a = nsa_mixer(q, kc, vc, ksl, vsl, kw, vw, gates, rel_bias,
                      cmp_pe_k[l], cmp_w1_k[l], cmp_b1_k[l], cmp_w2_k[l],
                      cmp_pe_v[l], cmp_w1_v[l], cmp_b1_v[l], cmp_w2_v[l])
        s = s5_mixer(u_ssm, ssm_lam_re[l], ssm_lam_im[l], ssm_log_step[l], ssm_b_re[l], ssm_b_im[l],
                     ssm_c_re[l], ssm_c_im[l], ssm_d[l], glu_w[l], glu_b[l])
        h = h + jnp.concatenate([a, s], axis=-1) @ w_out[l]
        h = h + 0.5 * swiglu(rmsnorm(h, ffn2_norm[l]), ffn2_w1[l], ffn2_w3[l], ffn2_w2[l])
    return rmsnorm(h, final_norm)
```

```python
import types
from contextlib import ExitStack
from concourse.bass_utils import run_bass_kernel_spmd

import numpy as np
import concourse.bass as bass
import concourse.mybir as mybir

F32 = mybir.dt.float32
BF16 = mybir.dt.bfloat16
I32 = mybir.dt.int32
AF = mybir.ActivationFunctionType
ALU = mybir.AluOpType
AX = mybir.AxisListType

CHUNK = 16000
NDMASEM = 48


class Buf:
    __slots__ = ("name", "lw", "rd")

    def __init__(self, name=""):
        self.name = name
        self.lw = None
        self.rd = []


class Ctx:
    def __init__(self, nc, es, needed=None):
        self.needed = needed
        self.rec = set()
        self.nc = nc
        self.es = es
        self.eng = {"pe": nc.tensor, "act": nc.scalar, "dve": nc.vector, "pool": nc.gpsimd, "sp": nc.sync}
        self.count = {e: 0 for e in self.eng}
        self.sems = {e: [] for e in self.eng}
        self.waited = {e: {} for e in self.eng}
        self.dsem = [es.enter_context(nc.semaphore(f"dq{i}")) for i in range(NDMASEM)]
        self.dval = [0] * NDMASEM
        self.dlast = [None] * NDMASEM
        self.dnext = {e: 0 for e in self.eng}
        self.n_instr = 0
        self.sig = {}
        self.nsig = {e: 0 for e in self.eng}
        self.drained = {e: 0 for e in self.eng}

    def _sem_for(self, e, idx):
        k = (idx - 1) // CHUNK
        while len(self.sems[e]) <= k:
            self.sems[e].append(self.es.enter_context(self.nc.semaphore(f"s_{e}_{len(self.sems[e])}")))
        return self.sems[e][k], (idx - 1) % CHUNK + 1

    def _wait(self, e, tok, raw=True):
        if tok is None:
            return
        if tok[0] == "dma":
            _, k, val = tok
            key = ("dma", k)
            if self.waited[e].get(key, 0) >= val:
                return
            self.eng[e].wait_ge(self.dsem[k], val)
            self.waited[e][key] = val
        else:
            e2, idx = tok
            if e2 == e and (e in ("pe", "pool") or not raw):
                return
            if self.waited[e].get(e2, 0) >= idx:
                return
            self.rec.add((e2, idx))
            if self.needed is None:
                sem, val = self._sem_for(e2, idx)
            else:
                sem, val = self._sem_for(e2, self.sig[(e2, idx)])
            self.eng[e].wait_ge(sem, val)
            self.waited[e][e2] = idx
        self.n_instr += 1

    def _deps(self, e, reads, writes):
        for b in reads:
            self._wait(e, b.lw, True)
        for b in writes:
            self._wait(e, b.lw, False)
            for t in b.rd:
                self._wait(e, t, False)

    def _commit(self, tok, reads, writes):
        for b in reads:
            b.rd.append(tok)
        for b in writes:
            b.lw = tok
            b.rd = []

    def op(self, e, fn, reads=(), writes=()):
        self._deps(e, reads, writes)
        ins = fn(self.eng[e])
        self.count[e] += 1
        idx = self.count[e]
        if self.needed is None:
            sem, _ = self._sem_for(e, idx)
            ins.then_inc(sem, 1)
        elif (e, idx) in self.needed:
            self.nsig[e] += 1
            self.sig[(e, idx)] = self.nsig[e]
            sem, _ = self._sem_for(e, self.nsig[e])
            ins.then_inc(sem, 1)
        self._commit((e, idx), reads, writes)
        self.n_instr += 1
        return ins

    def dma(self, e, out, in_, reads=(), writes=(), **kw):
        half = NDMASEM // 2
        base = half if e == "pool" else 0
        k = base + self.dnext[e]
        self.dnext[e] = (self.dnext[e] + 1) % half
        self._wait(e, self.dlast[k])
        self._deps(e, reads, writes)
        ins = self.eng[e].dma_start(out=out, in_=in_, **kw)
        ins.then_inc(self.dsem[k], 16)
        self.dval[k] += 16
        tok = ("dma", k, self.dval[k])
        self.dlast[k] = tok
        self._commit(tok, reads, writes)
        self.n_instr += 1
        return ins

    def barrier(self):
        for e in self.eng:
            for e2 in self.eng:
                if e2 != e and self.count[e2] > 0:
                    self._wait(e, (e2, self.count[e2]))
            for k in range(NDMASEM):
                self._wait(e, self.dlast[k])

    def final_wait(self, e="sp"):
        for k in range(NDMASEM):
            self._wait(e, self.dlast[k])
        for e2 in self.eng:
            if e2 != e and self.count[e2] > 0:
                self._wait(e, (e2, self.count[e2]))

from contextlib import ExitStack

D_MODEL = 2048; T = 2048; D_FF = 5632; KT = 16; FT = 44
TT = 512
NSUB = TT // 128
EPS = 1e-6


_SBT_N = [0]


def sbt(nc, es, name, shape, dtype):
    _SBT_N[0] += 1
    return es.enter_context(nc.sbuf_tensor(f"{name}_{_SBT_N[0]}", list(shape), dtype))


class WStream:
    def __init__(self, c, pools, items, depth=1, eng="pool"):
        self.c = c; self.pools = pools; self.items = items; self.depth = depth; self.eng = eng
        self.issued = 0
        self.pcount = {p: 0 for p in pools}
        self.loc = {}

    def _issue(self, i):
        it = self.items[i]
        pool, src, dst_sl = it[0], it[1], it[2]
        cache, fill = (it[3], it[4]) if len(it) > 3 else (None, False)
        tens, bufs = self.pools[pool]
        j = self.pcount[pool] % len(tens)
        self.pcount[pool] += 1
        dst = tens[j][:] if dst_sl is None else dst_sl(tens[j])
        if cache is None or fill:
            self.c.dma(self.eng, dst, src, writes=[bufs[j]])
            if cache is not None:
                self.c.dma("sp", cache, dst, reads=[bufs[j]])
        else:
            self.c.dma(self.eng, dst, cache, writes=[bufs[j]])
        self.loc[i] = (tens[j], bufs[j])

    def get(self, i):
        while self.issued < min(len(self.items), i + 1 + self.depth):
            self._issue(self.issued)
            self.issued += 1
        return self.loc.pop(i)


def rmsnorm_fm(c, S, gain_t, gk, out32=False):
    nc = c.nc
    psn, psn_b = S.ps["norm"]
    for dk in range(KT):
        c.op("act", lambda e, dk=dk: e.activation(out=S.sq[:, dk, :], in_=S.xT[:, dk, :], func=AF.Square),
             reads=[S.xT_b[dk]], writes=[S.sq_b[dk]])
    for dk in range(KT):
        c.op("pe", lambda e, dk=dk: e.matmul(psn[:, :TT], lhsT=S.ones_bf[:, :], rhs=S.sq[:, dk, :],
                                            start=(dk == 0), stop=(dk == KT - 1)),
             reads=[S.sq_b[dk], S.const_b], writes=[psn_b])
    c.op("dve", lambda e: e.tensor_scalar(out=S.rstd[:, :], in0=psn[:, :TT], scalar1=1.0 / D_MODEL, scalar2=EPS,
                                          op0=ALU.mult, op1=ALU.add), reads=[psn_b], writes=[S.rstd_b])
    c.op("act", lambda e: e.activation(out=S.rstd[:, :], in_=S.rstd[:, :], func=AF.Sqrt),
         reads=[S.rstd_b], writes=[S.rstd_b])
    c.op("dve", lambda e: e.reciprocal(out=S.rstd[:, :], in_=S.rstd[:, :]), reads=[S.rstd_b], writes=[S.rstd_b])
    ot, ob = (S.xT, S.xT_b) if out32 else (S.xn, S.xn_b)
    for dk in range(KT):
        c.op("dve", lambda e, dk=dk: e.scalar_tensor_tensor(
            out=ot[:, dk, :], in0=S.xT[:, dk, :], scalar=gain_t[:, gk * KT + dk:gk * KT + dk + 1],
            in1=S.rstd[:, :], op0=ALU.mult, op1=ALU.mult),
            reads=[S.xT_b[dk], S.rstd_b, S.const_b], writes=[ob[dk]])


def ffn_items(w1, w3, w2, cache=None, fill=False):
    items = []
    w1v = w1.rearrange("(kt p) f -> p kt f", p=128)
    w3v = w3.rearrange("(kt p) f -> p kt f", p=128)
    w2v = w2.rearrange("(ft p) d -> p ft d", p=128)
    for ci in range(FT // 4):
        items.append(("wa", w1v[:, :, ci * 512:(ci + 1) * 512], None) + ((cache[0][ci], fill) if cache else ()))
        items.append(("wb", w3v[:, :, ci * 512:(ci + 1) * 512], None) + ((cache[1][ci], fill) if cache else ()))
    for dt in range(KT):
        items.append(("wc", w2v[:, :, dt * 128:(dt + 1) * 128], None) + ((cache[2][dt], fill) if cache else ()))
    return items


def ffn_fm(c, S, ws, base):
    g1 = [S.ps["g1a"], S.ps["g1b"]]; g3 = [S.ps["g3a"], S.ps["g3b"]]
    for ci in range(FT // 4):
        wa, wa_b = ws.get(base + 2 * ci)
        wb, wb_b = ws.get(base + 2 * ci + 1)
        for j in range(4):
            ft = 4 * ci + j
            p1, p1b = g1[ft % 2]; p3, p3b = g3[ft % 2]
            for kt in range(KT):
                c.op("pe", lambda e, kt=kt, wa=wa, j=j, p1=p1: e.matmul(
                    p1[:, :TT], lhsT=wa[:, kt, j * 128:(j + 1) * 128], rhs=S.xn[:, kt, :],
                    start=(kt == 0), stop=(kt == KT - 1)), reads=[wa_b, S.xn_b[kt]], writes=[p1b])
            for kt in range(KT):
                c.op("pe", lambda e, kt=kt, wb=wb, j=j, p3=p3: e.matmul(
                    p3[:, :TT], lhsT=wb[:, kt, j * 128:(j + 1) * 128], rhs=S.xn[:, kt, :],
                    start=(kt == 0), stop=(kt == KT - 1)), reads=[wb_b, S.xn_b[kt]], writes=[p3b])
            st, stb = S.silu[ft % 2]
            c.op("act", lambda e, st=st, p1=p1: e.activation(out=st[:, :], in_=p1[:, :TT], func=AF.Silu),
                 reads=[p1b], writes=[stb])
            c.op("dve", lambda e, st=st, p3=p3, ft=ft: e.tensor_tensor(
                out=S.act[:, ft, :], in0=st[:, :], in1=p3[:, :TT], op=ALU.mult),
                reads=[stb, p3b], writes=[S.act_b[ft]])
    base += FT // 2
    yp = [S.ps["ya"], S.ps["yb"]]
    for dt in range(KT):
        wc, wc_b = ws.get(base + dt)
        py, pyb = yp[dt % 2]
        for ft in range(FT):
            c.op("pe", lambda e, ft=ft, wc=wc, py=py: e.matmul(
                py[:, :TT], lhsT=wc[:, ft, :], rhs=S.act[:, ft, :], start=(ft == 0), stop=(ft == FT - 1)),
                reads=[wc_b, S.act_b[ft]], writes=[pyb])
        c.op("dve", lambda e, dt=dt, py=py: e.scalar_tensor_tensor(
            out=S.xT[:, dt, :], in0=py[:, :TT], scalar=0.5, in1=S.xT[:, dt, :], op0=ALU.mult, op1=ALU.add),
            reads=[pyb, S.xT_b[dt]], writes=[S.xT_b[dt]])
    return base + KT

import types

PSN = ["g1a", "g1b", "g3a", "g3b", "ya", "yb", "norm", "tr"]
NT_FM = 25


def alloc_AD(c, nc, es, G):
    S = types.SimpleNamespace()
    S.ps = {n: G.psum[i] for i, n in enumerate(PSN)}
    S.xT = sbt(nc, es, "xT", [128, KT, TT], F32); S.xT_b = [Buf() for _ in range(KT)]
    S.xn = sbt(nc, es, "xn", [128, KT, TT], BF16); S.xn_b = [Buf() for _ in range(KT)]
    S.act = sbt(nc, es, "act", [128, FT, TT], BF16); S.act_b = [Buf() for _ in range(FT)]
    S.sq = S.act; S.sq_b = S.act_b
    S.rstd = sbt(nc, es, "rstd", [128, TT], F32); S.rstd_b = Buf()
    S.silu = [(sbt(nc, es, f"silu{i}", [128, TT], F32), Buf()) for i in range(2)]
    S.xs = sbt(nc, es, "xs", [128, D_MODEL], F32); S.xs_b = Buf()
    S.stg = [(sbt(nc, es, f"stg{i}", [128, 512], BF16), Buf()) for i in range(4)]
    S.stg32 = (sbt(nc, es, "stg32", [128, 512], F32), Buf())
    S.pools = {
        "wa": ([sbt(nc, es, f"wa{i}", [128, KT, 512], BF16) for i in range(2)], [Buf() for _ in range(2)]),
        "wb": ([sbt(nc, es, f"wb{i}", [128, KT, 512], BF16) for i in range(2)], [Buf() for _ in range(2)]),
        "wc": ([sbt(nc, es, f"wc{i}", [128, FT, 128], BF16) for i in range(2)], [Buf() for _ in range(2)]),
    }
    S.ones_bf = G.ones_bf; S.ident32 = G.ident32; S.gains = G.gains; S.const_b = G.const_b
    return S


def phase_A(c, nc, G, Dr):
    with ExitStack() as es:
        S = alloc_AD(c, nc, es, G)
        items = []
        wfm = Dr["w_in_fm"].rearrange("(kt p) n -> p kt n", p=128)
        wv = Dr["w_in_v"].rearrange("(kt p) n -> p kt n", p=128)
        for tt in range(T // TT):
            fill = (tt == 0)
            items += ffn_items(Dr["ffn1_w1"], Dr["ffn1_w3"], Dr["ffn1_w2"],
                               (Dr["c1_w1"], Dr["c1_w3"], Dr["c1_w2"]), fill)
            for ci in range(6):
                items.append(("wa", wfm[:, :, ci * 512:(ci + 1) * 512], None, Dr["c_win"][ci], fill))
            items.append(("wa", wfm[:, :, 3072:3200], lambda t: t[:, :, 0:128], Dr["c_wing"][:, :, :], fill))
            items.append(("wb", wv[:, :, :], None, Dr["c_winv"][:, :, :], fill))
        ws = WStream(c, S.pools, items)
        base = 0
        ptr, ptr_b = S.ps["tr"]
        h1v = Dr["h1T"].rearrange("(k p) t -> p k t", p=128)
        rot = [S.ps[n] for n in ("g1a", "g1b", "g3a", "g3b")]
        ev = 0
        for tt in range(T // TT):
            t0 = tt * TT
            for s in range(NSUB):
                c.dma("sp", S.xs[:, :], Dr["x"][t0 + s * 128:t0 + (s + 1) * 128, :], writes=[S.xs_b])
                for i in range(4):
                    for j in range(4):
                        dk = 4 * i + j
                        c.op("pe", lambda e, dk=dk, j=j: e.transpose(
                            out=ptr[:, j * 128:(j + 1) * 128], in_=S.xs[:, dk * 128:(dk + 1) * 128],
                            identity=S.ident32[:, :]), reads=[S.xs_b, S.const_b], writes=[ptr_b])
                    c.op("dve", lambda e, i=i, s=s: e.tensor_copy(
                        out=S.xT[:, 4 * i:4 * i + 4, s * 128:(s + 1) * 128],
                        in_=ptr[:, :].rearrange("p (j t) -> p j t", j=4)),
                        reads=[ptr_b], writes=S.xT_b[4 * i:4 * i + 4])
            rmsnorm_fm(c, S, S.gains, 0)
            base = ffn_fm(c, S, ws, base)
            c.dma("sp", h1v[:, :, t0:t0 + TT], S.xT[:, :, :], reads=S.xT_b)
            rmsnorm_fm(c, S, S.gains, 1)
            for ci in range(7):
                wa, wa_b = ws.get(base + ci)
                for j in range(4 if ci < 6 else 1):
                    nt = 4 * ci + j
                    pp, ppb = rot[nt % 4]
                    for kt in range(KT):
                        c.op("pe", lambda e, kt=kt, wa=wa, j=j, pp=pp: e.matmul(
                            pp[:, :TT], lhsT=wa[:, kt, j * 128:(j + 1) * 128], rhs=S.xn[:, kt, :],
                            start=(kt == 0), stop=(kt == KT - 1)), reads=[wa_b, S.xn_b[kt]], writes=[ppb])
                    eng = "act" if ev % 2 == 0 else "dve"
                    if nt < 24:
                        st, stb = S.stg[ev % 4]
                        dst = Dr["projT"][nt * 128:(nt + 1) * 128, t0:t0 + TT]
                    else:
                        st, stb = S.stg32
                        dst = Dr["gT"][:, t0:t0 + TT]
                    if eng == "act":
                        c.op("act", lambda e, st=st, pp=pp: e.copy(out=st[:, :TT], in_=pp[:, :TT]),
                             reads=[ppb], writes=[stb])
                    else:
                        c.op("dve", lambda e, st=st, pp=pp: e.tensor_copy(out=st[:, :TT], in_=pp[:, :TT]),
                             reads=[ppb], writes=[stb])
                    ev += 1
                    c.dma("sp", dst, st[:, :TT], reads=[stb])
            base += 7
            wv0, wv0_b = ws.get(base)
            base += 1
            for s in range(NSUB):
                pp, ppb = rot[s % 4]
                for kt in range(KT):
                    c.op("pe", lambda e, kt=kt, s=s, pp=pp: e.matmul(
                        pp[:, :], lhsT=S.xn[:, kt, s * 128:(s + 1) * 128],
                        rhs=wv0[:, kt, :], start=(kt == 0), stop=(kt == KT - 1)),
                        reads=[wv0_b, S.xn_b[kt]], writes=[ppb])
                st, stb = S.stg[ev % 4]; ev += 1
                c.op("dve", lambda e, st=st, pp=pp: e.tensor_copy(out=st[:, :], in_=pp[:, :]),
                     reads=[ppb], writes=[stb])
                c.dma("sp", Dr["vtm"][t0 + s * 128:t0 + (s + 1) * 128, :], st[:, :], reads=[stb])
        c.barrier()


def phase_D(c, nc, G, Dr):
    with ExitStack() as es:
        S = alloc_AD(c, nc, es, G)
        items = []
        wo = Dr["w_out"].rearrange("(kt p) n -> p kt n", p=128)
        for tt in range(T // TT):
            fill = (tt == 0)
            for ci in range(4):
                items.append(("wa", wo[:, :, ci * 512:(ci + 1) * 512], None, Dr["c_wout"][ci], fill))
            items += ffn_items(Dr["ffn2_w1"], Dr["ffn2_w3"], Dr["ffn2_w2"],
                               (Dr["c2_w1"], Dr["c2_w3"], Dr["c2_w2"]), fill)
        ws = WStream(c, S.pools, items)
        base = 0
        ptr, ptr_b = S.ps["tr"]
        h1v = Dr["h1T"].rearrange("(k p) t -> p k t", p=128)
        mixv = Dr["mixT"].rearrange("(k p) t -> p k t", p=128)
        rot = [S.ps[n] for n in ("g1a", "g1b", "g3a", "g3b")]
        for tt in range(T // TT):
            t0 = tt * TT
            c.dma("sp", S.xT[:, :, :], h1v[:, :, t0:t0 + TT], writes=S.xT_b)
            c.dma("sp", S.xn[:, :, :], mixv[:, :, t0:t0 + TT], writes=S.xn_b)
            for ci in range(4):
                wa, wa_b = ws.get(base + ci)
                for j in range(4):
                    dt = 4 * ci + j
                    pp, ppb = rot[dt % 4]
                    for kt in range(KT):
                        c.op("pe", lambda e, kt=kt, wa=wa, j=j, pp=pp: e.matmul(
                            pp[:, :TT], lhsT=wa[:, kt, j * 128:(j + 1) * 128], rhs=S.xn[:, kt, :],
                            start=(kt == 0), stop=(kt == KT - 1)), reads=[wa_b, S.xn_b[kt]], writes=[ppb])
                    c.op("dve", lambda e, dt=dt, pp=pp: e.tensor_tensor(
                        out=S.xT[:, dt, :], in0=S.xT[:, dt, :], in1=pp[:, :TT], op=ALU.add),
                        reads=[ppb, S.xT_b[dt]], writes=[S.xT_b[dt]])
            base += 4
            rmsnorm_fm(c, S, S.gains, 2)
            base = ffn_fm(c, S, ws, base)
            rmsnorm_fm(c, S, S.gains, 3, out32=True)
            for s in range(NSUB):
                for i in range(4):
                    for j in range(4):
                        dk = 4 * i + j
                        c.op("pe", lambda e, dk=dk, j=j, s=s: e.transpose(
                            out=ptr[:, j * 128:(j + 1) * 128], in_=S.xT[:, dk, s * 128:(s + 1) * 128],
                            identity=S.ident32[:, :]), reads=[S.xT_b[dk], S.const_b], writes=[ptr_b])
                    c.op("dve", lambda e, i=i: e.tensor_copy(out=S.xs[:, i * 512:(i + 1) * 512], in_=ptr[:, :]),
                         reads=[ptr_b], writes=[S.xs_b])
                c.dma("sp", Dr["out"][t0 + s * 128:t0 + (s + 1) * 128, :], S.xs[:, :], reads=[S.xs_b])
        c.barrier()

HD = 128; NH = 8; NG = 2; NCMP = 127; NBLK = 32
SCALE = HD ** -0.5
NEG = -30000.0
LSW = 1536
LC = 4096
YW = 1408


def _t5_bucket_np(dist):
    import math
    n = np.maximum(dist, 0)
    nf = np.maximum(n, 1).astype(np.float32)
    large = 16 + (np.log(nf / np.float32(16)) / np.float32(math.log(128 / 16)) * np.float32(16)).astype(np.int32)
    large = np.minimum(large, 31)
    return np.where(n < 16, n, large)


def nsa_consts():
    k = {}
    def onehot(d, masked):
        oh = np.zeros((33, d.shape[0]), np.float32)
        b = _t5_bucket_np(d)
        for i in range(d.shape[0]):
            if masked[i]:
                oh[32, i] = 1.0
            else:
                oh[b[i], i] = 1.0
        return oh
    d = np.arange(LSW) - 511
    k["oh_sel"] = onehot(d, d < 0)
    k["oh_win"] = onehot(d, (d < 0) | (d >= 512))
    d = np.arange(LC) - 2047
    k["oh_cmp"] = onehot(d, d < 0)
    k["J128"] = np.eye(128, dtype=np.float32)[::-1].copy()
    j127 = np.zeros((128, 128), np.float32); j127[:127, :127] = np.eye(127, dtype=np.float32)[::-1]
    k["J127"] = j127
    cs = np.arange(NCMP) * 16; js = np.arange(NBLK) * 64
    ov = np.clip(np.minimum(cs[:, None] + 32, js[None, :] + 64) - np.maximum(cs[:, None], js[None, :]), 0, None)
    ovl = np.zeros((128, 33), np.float32); ovl[:NCMP, :32] = ov.astype(np.float32) / 32.0; ovl[:NCMP, 32] = 1.0
    k["ovl"] = ovl
    t = np.arange(T); jj = np.arange(NBLK); cur = t // 64
    forced = (jj[None, :] == 0) | (jj[None, :] == cur[:, None]) | (jj[None, :] == cur[:, None] - 1)
    causal = js[None, :] <= t[:, None]
    cm = (causal & ~forced).astype(np.float32)
    ad = np.where(forced, 1e6, np.where(causal, 0.0, -1e9)).astype(np.float32)
    k["tk_mul"] = np.ascontiguousarray(cm.reshape(16, 128, 32).transpose(1, 0, 2))
    k["tk_add"] = np.ascontiguousarray(ad.reshape(16, 128, 32).transpose(1, 0, 2))
    ex = np.zeros((32, 16, 128), np.float32)
    for kt in range(16):
        for kk in range(128):
            ex[2 * kt + kk // 64, kt, kk] = -32768.0
    k["expneg"] = ex
    sg = np.zeros((24, 24, 128), np.float32)
    for r in range(24):
        sg[r, r, :] = 1.0
    k["selg"] = sg
    return k


def nsa_inputs(din):
    din("oh_sel", [33, LSW]); din("oh_win", [33, LSW]); din("oh_cmp", [33, LC])
    din("J128", [128, 128]); din("J127", [128, 128]); din("ovl", [128, 33])
    din("tk_mul", [128, 16, 32]); din("tk_add", [128, 16, 32]); din("expneg", [32, 16, 128]); din("selg", [24, 24, 128])
    din("rel_bias", [32, 8])
    for kv in ("k", "v"):
        din(f"cmp_w1_{kv}", [4096, 128]); din(f"cmp_w2_{kv}", [128, 128])
        din(f"cmp_peT_{kv}", [128, 32]); din(f"cmp_b1_{kv}", [128, 1])


def nsa_scratch(dscr):
    dscr("Fsel", [8, LSW], F32); dscr("Fwin", [8, LSW], F32); dscr("Fcmp", [8, LC], F32)


def nsa_host(m, inp):
    m.update(nsa_consts())
    m["rel_bias"] = np.ascontiguousarray(np.asarray(inp["rel_bias"], np.float32))
    for kv in ("k", "v"):
        m[f"cmp_w1_{kv}"] = np.ascontiguousarray(np.asarray(inp[f"cmp_w1_{kv}"])[0])
        m[f"cmp_w2_{kv}"] = np.ascontiguousarray(np.asarray(inp[f"cmp_w2_{kv}"])[0])
        m[f"cmp_peT_{kv}"] = np.ascontiguousarray(np.asarray(inp[f"cmp_pe_{kv}"])[0].T)
        m[f"cmp_b1_{kv}"] = np.ascontiguousarray(np.asarray(inp[f"cmp_b1_{kv}"])[0].reshape(128, 1))


def phase_NSA(c, nc, G, Dr):
    with ExitStack() as es:
        def tb(name, shape, dt):
            return sbt(nc, es, name, shape, dt), Buf(name)
        PS = {n: G.psum[i] for i, n in enumerate(("s0", "s1", "o0", "o1", "r0", "r1", "gb", "misc"))}
        cb = Buf("nsa_const")
        ones32, _ = tb("ones32", [128, 128], F32)
        c.op("dve", lambda e: e.memset(ones32[:, :], 1.0), writes=[cb])
        def load_const(name, shape, dt=F32, eng="sp", src=None):
            t_, _ = tb(name, shape, dt)
            c.dma(eng, t_[tuple(slice(None) for _ in shape)], src if src is not None else Dr[name][tuple(slice(None) for _ in shape)], writes=[cb])
            return t_
        J128 = load_const("J128", [128, 128]); J127 = load_const("J127", [128, 128])
        ovl = load_const("ovl", [128, 33])
        tkm = load_const("tk_mul", [128, 16, 32]); tka = load_const("tk_add", [128, 16, 32])
        expneg = load_const("expneg", [32, 16, 128], BF16, eng="pool")
        selg = load_const("selg", [24, 24, 128])
        cfar = load_const("cfar", [128, 8], src=Dr["rel_bias"][31:32, :].partition_broadcast(128))
        text, _ = tb("text", [33, 8], F32)
        c.op("dve", lambda e: e.memset(text[32:33, :], NEG), writes=[cb])
        c.dma("sp", text[0:32, :], Dr["rel_bias"][:, :], writes=[cb])
        w1 = {}; w2 = {}; peT = {}; b1 = {}
        for kv in ("k", "v"):
            w1[kv] = load_const(f"w1{kv}", [128, 32, 128], BF16, eng="pool",
                                src=Dr[f"cmp_w1_{kv}"].rearrange("(l d) j -> d l j", d=128))
            peT[kv] = load_const(f"peT{kv}", [128, 32], BF16, eng="pool", src=Dr[f"cmp_peT_{kv}"][:, :])
            b1[kv] = load_const(f"b1{kv}", [128, 1], src=Dr[f"cmp_b1_{kv}"][:, :])
        w2["k"] = load_const("w2k", [128, 128], BF16, eng="pool", src=Dr["cmp_w2_k"][:, :])
        w2["v"] = load_const("w2v", [128, 128], BF16, eng="pool", src=Dr["cmp_w2_v"][:, :])

        ohs = [tb(f"oh{i}", [33, 512], F32) for i in range(2)]
        fsb = [tb(f"fsb{i}", [8, 512], F32) for i in range(2)]
        pm, pmb = PS["misc"]
        nf = 0
        for nm, scr, L in (("oh_sel", "Fsel", LSW), ("oh_win", "Fwin", LSW), ("oh_cmp", "Fcmp", LC)):
            for ch in range(L // 512):
                oh, ohb = ohs[nf % 2]
                c.dma("sp", oh[:, :], Dr[nm][:, ch * 512:(ch + 1) * 512], writes=[ohb])
                c.op("pe", lambda e, oh=oh: e.matmul(pm[0:8, :], lhsT=text[:, :], rhs=oh[:, :],
                                                     start=True, stop=True), reads=[ohb, cb], writes=[pmb])
                ft_, ftb = fsb[nf % 2]; nf += 1
                c.op("dve", lambda e, ft_=ft_: e.tensor_copy(out=ft_[:, :], in_=pm[0:8, :]), reads=[pmb], writes=[ftb])
                c.dma("sp", Dr[scr][:, ch * 512:(ch + 1) * 512], ft_[:, :], reads=[ftb], writes=[cb])

        gsig, gsb = tb("gsig", [24, T], F32)
        c.dma("sp", gsig[:, :], Dr["gT"][0:24, :], writes=[gsb])
        c.op("act", lambda e: e.activation(out=gsig[:, :], in_=gsig[:, :], func=AF.Sigmoid), reads=[gsb], writes=[gsb])

        qT = [tb(f"qT{i}", [128, T], BF16) for i in range(4)]
        kT = {n: tb(f"kT{n}", [128, T], BF16) for n in ("c", "vc", "s", "w")}
        vtm = {n: tb(f"vtm{n}", [128, 16, 128], BF16) for n in ("s", "w")}
        kcbT, kcbb = tb("kcbT", [128, 128], BF16)
        vcb, vcbb = tb("vcb", [128, 128], F32)
        hid, hidb = tb("hid", [128, 128], BF16)
        bcol, bcolb = tb("bcol", [128, 1], F32)
        aacc, _ = tb("ccmp", [128, 4, T], BF16)
        aacc_b = [[Buf() for _ in range(4)] for _ in range(4)]
        acc32, acc32b = tb("acc32", [128, 512], F32)
        abf = [tb(f"abf{i}", [128, T], BF16) for i in range(1)]
        impa, impab = tb("impa", [128, 16, 32], F32)
        imp2, imp2b = tb("imp2", [128, 16, 32], F32)
        wk2, wk2b = tb("wk2", [128, 32], F32)
        m8, m8b = tb("m8", [128, 16], F32)
        nsel, nselb = tb("nsel", [128, 32], F32)
        nselT, nselTb = tb("nselT", [32, T], BF16)
        rec4, rec4b = tb("rec4", [128, 4], F32)
        hk, hkb = tb("hk", [128, 2048], F32)
        sstrip = [tb(f"sstrip{i}", [128, YW], F32) for i in range(2)]
        wstrip = [tb(f"wstrip{i}", [128, YW], F32) for i in range(2)]
        cstrip = [tb(f"cstrip{i}", [128, T], F32) for i in range(2)]
        tmpf = [tb(f"tmpf{i}", [128, 512], F32) for i in range(5)]
        pbf = [tb(f"pbf{i}", [128, 512], BF16) for i in range(8)]
        pf32 = [tb(f"pf32{i}", [128, 512], F32) for i in range(3)]
        rr, rrb = tb("rr", [128, 512], F32)
        wgt, wgtb = tb("wgt", [128, 512], F32)
        ctr = {"tmp": 0, "p": 0, "pf": 0, "s": 0, "or": 0}

        def rotp(lst, key):
            r = lst[ctr[key] % len(lst)]; ctr[key] += 1
            return r

        def combine(h, hl, qc, br, po, pob, pr, prb, first):
            pg, pgb = PS["gb"]
            r = 3 * h + br
            c.op("pe", lambda e: e.matmul(pg[:, :], lhsT=selg[:, r, :], rhs=gsig[:, qc * 512:(qc + 1) * 512],
                                          start=True, stop=True), reads=[cb, gsb], writes=[pgb])
            c.op("dve", lambda e: e.tensor_scalar_max(out=rr[:, :], in0=pr[:, :], scalar1=1e-30), reads=[prb], writes=[rrb])
            c.op("dve", lambda e: e.reciprocal(out=rr[:, :], in_=rr[:, :]), reads=[rrb], writes=[rrb])
            c.op("dve", lambda e: e.tensor_tensor(out=wgt[:, :], in0=rr[:, :], in1=pg[:, :], op=ALU.mult),
                 reads=[rrb, pgb], writes=[wgtb])
            dst = aacc[:, hl, qc * 512:(qc + 1) * 512]
            if br == 0:
                c.op("dve", lambda e: e.tensor_tensor(out=dst, in0=wgt[:, :], in1=po[:, :], op=ALU.mult),
                     reads=[wgtb, pob], writes=[aacc_b[hl][qc]])
            elif br == 2:
                c.op("dve", lambda e: e.tensor_tensor(out=acc32[:, :], in0=wgt[:, :], in1=po[:, :], op=ALU.mult),
                     reads=[wgtb, pob], writes=[acc32b])
            else:
                ab_, abb_ = abf[0]
                c.op("dve", lambda e: e.tensor_tensor(out=wgt[:, :], in0=wgt[:, :], in1=po[:, :], op=ALU.mult),
                     reads=[wgtb, pob], writes=[wgtb])
                c.op("pool", lambda e: e.tensor_tensor(out=acc32[:, :], in0=acc32[:, :], in1=wgt[:, :], op=ALU.add),
                     reads=[wgtb, acc32b], writes=[acc32b])
                c.op("dve", lambda e: e.tensor_tensor(out=ab_[:, qc * 512:(qc + 1) * 512], in0=acc32[:, :], in1=dst, op=ALU.add),
                     reads=[acc32b, aacc_b[hl][qc]], writes=[abb_])

        def build_strip(F_ap_tensor, h, L, cstep, nrows, width, Jt, dst, dstb):
            src = bass.AP(tensor=F_ap_tensor, offset=h * L, ap=[[cstep, nrows], [1, width]])
            c.dma("sp", hk[0:nrows, 0:width], src, reads=[cb], writes=[hkb])
            o = 0
            while o < width:
                n = min(512, width - o)
                ps_, psb_ = rotp([PS["s0"], PS["s1"]], "s")
                c.op("pe", lambda e, o=o, n=n, ps_=ps_: e.matmul(ps_[0:nrows, 0:n], lhsT=Jt[0:nrows, 0:nrows],
                                                                   rhs=hk[0:nrows, o:o + n], start=True, stop=True),
                     reads=[hkb, cb], writes=[psb_])
                c.op("act", lambda e, o=o, n=n, ps_=ps_: e.copy(out=dst[0:nrows, o:o + n], in_=ps_[0:nrows, 0:n]),
                     reads=[psb_], writes=[dstb])
                o += n

        nhead = 0
        for g in range(NG):
            for n, row in (("c", 1024), ("vc", 1280), ("s", 1536), ("w", 1792)):
                c.dma("sp", kT[n][0][:, :], Dr["projT"][row + g * 128:row + (g + 1) * 128, :], writes=[kT[n][1]])
            for n, col in (("s", 0), ("w", 256)):
                c.dma("sp", vtm[n][0][:, :, :],
                      Dr["vtm"].rearrange("(kt p) n -> p kt n", p=128)[:, :, col + g * 128:col + (g + 1) * 128],
                      writes=[vtm[n][1]])
            for hl in range(4):
                h = 4 * g + hl
                c.dma("sp", qT[hl][0][:, :], Dr["projT"][h * 128:(h + 1) * 128, :], writes=[qT[hl][1]])
            for kv, src in (("k", "c"), ("v", "vc")):
                kt_, ktb = kT[src]
                ph, phb = PS["misc"]
                for l in range(32):
                    c.op("pe", lambda e, l=l: e.matmul(ph[:, 0:NCMP], lhsT=w1[kv][:, l, :],
                                                      rhs=kt_[:, l:l + 16 * (NCMP - 1) + 1:16],
                                                      start=(l == 0), stop=(l == 31)), reads=[cb, ktb], writes=[phb])
                pb_, pbb_ = PS["gb"]
                for l in range(32):
                    c.op("pe", lambda e, l=l: e.matmul(pb_[:, 0:1], lhsT=w1[kv][:, l, :], rhs=peT[kv][:, l:l + 1],
                                                      start=(l == 0), stop=(l == 31)), reads=[cb], writes=[pbb_])
                c.op("dve", lambda e: e.tensor_tensor(out=bcol[:, :], in0=b1[kv][:, :], in1=pb_[:, 0:1], op=ALU.add),
                     reads=[cb, pbb_], writes=[bcolb])
                c.op("act", lambda e: e.activation(out=hid[:, 0:NCMP], in_=ph[:, 0:NCMP], func=AF.Gelu_apprx_tanh,
                                                   bias=bcol[:, 0:1]), reads=[phb, bcolb], writes=[hidb])
                po_, pob_ = PS["o0"]
                if kv == "k":
                    c.op("pe", lambda e: e.matmul(po_[:, 0:NCMP], lhsT=w2["k"][:, :], rhs=hid[:, 0:NCMP],
                                                  start=True, stop=True), reads=[cb, hidb], writes=[pob_])
                    c.op("dve", lambda e: e.tensor_copy(out=kcbT[:, 0:NCMP], in_=po_[:, 0:NCMP]),
                         reads=[pob_], writes=[kcbb])
                else:
                    c.op("pe", lambda e: e.matmul(po_[0:NCMP, 0:128], lhsT=hid[:, 0:NCMP], rhs=w2["v"][:, :],
                                                  start=True, stop=True), reads=[cb, hidb], writes=[pob_])
                    c.op("dve", lambda e: e.tensor_copy(out=vcb[0:NCMP, :], in_=po_[0:NCMP, 0:128]),
                         reads=[pob_], writes=[vcbb])
            c.op("pool", lambda e: e.memset(impa[:, :, :], 0.0), writes=[impab])
            p1pend = []

            def p1_stage1(h, hl, qc, cs_, csb_):
                q_, qb_ = qT[hl]
                ps_, psb_ = rotp([PS["s0"], PS["s1"]], "s")
                c.op("pe", lambda e: e.matmul(ps_[0:NCMP, :], lhsT=kcbT[:, 0:NCMP], rhs=q_[:, qc * 512:(qc + 1) * 512],
                                              start=True, stop=True), reads=[kcbb, qb_], writes=[psb_])
                tf, tfb = rotp(tmpf, "tmp")
                c.op("dve", lambda e: e.scalar_tensor_tensor(out=tf[0:NCMP, :], in0=ps_[0:NCMP, :], scalar=SCALE,
                                                             in1=cs_[0:NCMP, qc * 512:(qc + 1) * 512],
                                                             op0=ALU.mult, op1=ALU.add),
                     reads=[psb_, csb_], writes=[tfb])
                pf, pfb = rotp(pf32, "pf")
                c.op("act", lambda e: e.activation(out=pf[0:NCMP, :], in_=tf[0:NCMP, :], func=AF.Exp),
                     reads=[tfb], writes=[pfb])
                return (h, hl, qc, pf, pfb)

            def p1_stage2(h, hl, qc, pf, pfb):
                po, pob = PS["o0"] if ctr["or"] % 2 == 0 else PS["o1"]
                pr, prb = PS["r0"] if ctr["or"] % 2 == 0 else PS["r1"]
                ctr["or"] += 1
                c.op("pe", lambda e: e.matmul(po[:, :], lhsT=vcb[0:NCMP, :], rhs=pf[0:NCMP, :], start=True, stop=True),
                     reads=[vcbb, pfb], writes=[pob])
                c.op("pe", lambda e: e.matmul(pr[:, :], lhsT=ones32[0:NCMP, :], rhs=pf[0:NCMP, :], start=True, stop=True),
                     reads=[cb, pfb], writes=[prb])
                pi, pib = PS["misc"]
                for s in range(4):
                    c.op("pe", lambda e, s=s: e.matmul(pi[:, s * 33:(s + 1) * 33], lhsT=pf[0:NCMP, s * 128:(s + 1) * 128],
                                                      rhs=ovl[0:NCMP, :], start=True, stop=True),
                         reads=[cb, pfb], writes=[pib])
                combine(h, hl, qc, 0, po, pob, pr, prb, True)
                piv = pi[:, 0:132].rearrange("p (s j) -> p s j", j=33)
                c.op("dve", lambda e: e.tensor_scalar_max(out=rec4[:, :].rearrange("p (s o) -> p s o", o=1),
                                                          in0=piv[:, :, 32:33], scalar1=1e-30),
                     reads=[pib], writes=[rec4b])
                c.op("dve", lambda e: e.reciprocal(out=rec4[:, :], in_=rec4[:, :]), reads=[rec4b], writes=[rec4b])
                for s in range(4):
                    c.op("dve", lambda e, s=s: e.scalar_tensor_tensor(
                        out=impa[:, 4 * qc + s, :], in0=pi[:, s * 33:s * 33 + 32], scalar=rec4[:, s:s + 1],
                        in1=impa[:, 4 * qc + s, :], op0=ALU.mult, op1=ALU.add),
                        reads=[pib, rec4b, impab], writes=[impab])

            for hl in range(4):
                h = 4 * g + hl
                cs_, csb_ = cstrip[hl % 2]
                build_strip(Dr["Fcmp"].tensor, h, LC, 16, NCMP, T, J127, cs_, csb_)
                for qc in range(4):
                    p1pend.append(p1_stage1(h, hl, qc, cs_, csb_))
                    while len(p1pend) > 1:
                        p1_stage2(*p1pend.pop(0))
                nhead += 1
            while p1pend:
                p1_stage2(*p1pend.pop(0))
            c.op("dve", lambda e: e.tensor_tensor(out=imp2[:, :, :], in0=impa[:, :, :], in1=tkm[:, :, :], op=ALU.mult),
                 reads=[impab, cb], writes=[imp2b])
            c.op("dve", lambda e: e.tensor_tensor(out=imp2[:, :, :], in0=imp2[:, :, :], in1=tka[:, :, :], op=ALU.add),
                 reads=[imp2b, cb], writes=[imp2b])
            for tt in range(16):
                c.op("dve", lambda e: e.max(out=m8[:, 0:8], in_=imp2[:, tt, :]), reads=[imp2b], writes=[m8b])
                c.op("dve", lambda e: e.match_replace(out=wk2[:, :], in_to_replace=m8[:, 0:8], in_values=imp2[:, tt, :],
                                                      imm_value=-3e38), reads=[imp2b, m8b], writes=[wk2b])
                c.op("dve", lambda e: e.max(out=m8[:, 8:16], in_=wk2[:, :]), reads=[wk2b], writes=[m8b])
                c.op("dve", lambda e: e.tensor_scalar(out=nsel[:, :], in0=imp2[:, tt, :], scalar1=m8[:, 15:16], scalar2=None,
                                                      op0=ALU.is_lt), reads=[imp2b, m8b], writes=[nselb])
                pt, ptb = PS["misc"]
                c.op("pe", lambda e: e.transpose(out=pt[0:32, 0:128], in_=nsel[:, :], identity=G.ident32[:, :]),
                     reads=[nselb, G.const_b], writes=[ptb])
                c.op("act", lambda e: e.copy(out=nselT[:, tt * 128:(tt + 1) * 128], in_=pt[0:32, 0:128]),
                     reads=[ptb], writes=[nselTb])
            LOOK = 3
            sbanks = [PS["s0"], PS["s1"], PS["misc"]]
            pend = []
            grp = [0]

            def stage1(u):
                h, hl, qc, br, kt, ii, n, gi, strip_, stripb_ = u
                q_, qb_ = qT[hl]
                kname = "w" if br == 2 else "s"
                k_, kb_ = kT[kname]
                ps_, psb_ = rotp(sbanks, "s")
                c.op("pe", lambda e: e.matmul(ps_[:, :], lhsT=k_[:, kt * 128:(kt + 1) * 128],
                                              rhs=q_[:, qc * 512:(qc + 1) * 512], start=True, stop=(br == 2)),
                     reads=[kb_, qb_], writes=[psb_])
                if br == 1:
                    c.op("pe", lambda e: e.matmul(ps_[:, :], lhsT=expneg[:, kt, :], rhs=nselT[:, qc * 512:(qc + 1) * 512],
                                                  start=False, stop=True), reads=[cb, nselTb], writes=[psb_])
                pb_, pbb_ = rotp(pbf, "p")
                if br == 2 or kt >= 4 * qc - 1:
                    yoff = 512 * qc - 128 * kt + 384
                    tf, tfb = rotp(tmpf, "tmp")
                    c.op("dve", lambda e: e.scalar_tensor_tensor(out=tf[:, :], in0=ps_[:, :], scalar=SCALE,
                                                                 in1=strip_[:, yoff:yoff + 512], op0=ALU.mult, op1=ALU.add),
                         reads=[psb_, stripb_], writes=[tfb])
                    c.op("act", lambda e: e.activation(out=pb_[:, :], in_=tf[:, :], func=AF.Exp), reads=[tfb], writes=[pbb_])
                else:
                    c.op("act", lambda e: e.activation(out=pb_[:, :], in_=ps_[:, :], func=AF.Exp, bias=cfar[:, h:h + 1],
                                                       scale=SCALE), reads=[psb_, cb], writes=[pbb_])
                return (pb_, pbb_)

            def stage2(u, pp):
                h, hl, qc, br, kt, ii, n, gi, strip_, stripb_ = u
                pb_, pbb_ = pp
                v_, vb_ = vtm["w" if br == 2 else "s"]
                po, pob = PS["o0"] if gi % 2 == 0 else PS["o1"]
                pr, prb = PS["r0"] if gi % 2 == 0 else PS["r1"]
                c.op("pe", lambda e: e.matmul(po[:, :], lhsT=v_[:, kt, :], rhs=pb_[:, :], start=(ii == 0), stop=(ii == n - 1)),
                     reads=[vb_, pbb_], writes=[pob])
                c.op("pe", lambda e: e.matmul(pr[:, :], lhsT=G.ones_bf[:, :], rhs=pb_[:, :], start=(ii == 0), stop=(ii == n - 1)),
                     reads=[G.const_b, pbb_], writes=[prb])
                if ii == n - 1:
                    combine(h, hl, qc, br, po, pob, pr, prb, False)

            def flush(keep):
                while len(pend) > keep:
                    u, pp = pend.pop(0)
                    stage2(u, pp)

            for hl in range(4):
                h = 4 * g + hl
                ss_, ssb_ = sstrip[h % 2]; wsr_, wsb_ = wstrip[h % 2]
                build_strip(Dr["Fsel"].tensor, h, LSW, 1, 128, YW, J128, ss_, ssb_)
                build_strip(Dr["Fwin"].tensor, h, LSW, 1, 128, YW, J128, wsr_, wsb_)
                for qc in range(4):
                    for br, strip_, stripb_ in ((2, wsr_, wsb_), (1, ss_, ssb_)):
                        kts = list(range(max(0, 4 * qc - 4), 4 * qc + 4)) if br == 2 else list(range(0, 4 * qc + 4))
                        gi = grp[0]; grp[0] += 1
                        for ii, kt in enumerate(kts):
                            u = (h, hl, qc, br, kt, ii, len(kts), gi, strip_, stripb_)
                            pend.append((u, stage1(u)))
                            flush(LOOK)
                flush(0)
                ab_, abb_ = abf[0]
                c.dma("sp", Dr["mixT"][h * 128:(h + 1) * 128, :], ab_[:, :], reads=[abb_])
        c.barrier()

import math
TWO_PI = 2.0 * math.pi
NPAIR = 32
STAG = 14


def s5_inputs(din):
    for n in ("lre_b", "lim_b", "lstep_b", "bT_re", "bT_im"):
        din(n, [128, 8, 64])
    for n in ("lreP", "limP", "lstepP"):
        din(n, [128, NPAIR])
    din("cpad_re", [128, NPAIR, 128]); din("cpad_im", [128, NPAIR, 128])
    din("maskcol", [128, 8]); din("dcol", [128, 8]); din("glub_col", [128, 8]); din("iota512", [128, 512])
    din("glu_w", [1024, 1024])


def s5_host(m, inp):
    f = lambda n: np.asarray(inp[n], np.float32)[0]
    lre, lim, lst = f("ssm_lam_re"), f("ssm_lam_im"), f("ssm_log_step")
    bre, bim, cre, cim = f("ssm_b_re"), f("ssm_b_im"), f("ssm_c_re"), f("ssm_c_im")
    def lay_b(a):
        out = np.zeros((128, 8, 64), np.float32)
        for g in range(64):
            out[(g % 8) * 16:(g % 8) * 16 + 16, g // 8, :] = a[g][None, :]
        return out
    m["lre_b"] = lay_b(lre); m["lim_b"] = lay_b(lim); m["lstep_b"] = lay_b(np.repeat(lst[:, None], 64, axis=1))
    def lay_bT(b):
        out = np.zeros((128, 8, 64), np.float32)
        for g in range(64):
            out[(g % 8) * 16:(g % 8) * 16 + 16, g // 8, :] = b[g].T
        return out
    m["bT_re"] = lay_bT(bre); m["bT_im"] = lay_bT(bim)
    def lay_P(a):
        out = np.zeros((128, NPAIR), np.float32)
        for g in range(64):
            out[(g % 2) * 64:(g % 2) * 64 + 64, g // 2] = a[g]
        return out
    m["lreP"] = lay_P(lre); m["limP"] = lay_P(lim); m["lstepP"] = lay_P(np.repeat(lst[:, None], 64, axis=1))
    def lay_c(cc):
        out = np.zeros((128, NPAIR, 128), np.float32)
        for g in range(64):
            half = g % 2; g8 = g % 8
            out[half * 64:half * 64 + 64, g // 2, 16 * g8:16 * g8 + 16] = cc[g].T
        return out
    m["cpad_re"] = lay_c(cre); m["cpad_im"] = lay_c(cim)
    mc = np.zeros((128, 8), np.float32)
    for g8 in range(8):
        mc[g8 * 16:(g8 + 1) * 16, g8] = 1.0
    m["maskcol"] = mc
    m["dcol"] = np.ascontiguousarray(f("ssm_d").reshape(8, 128).T)
    m["glub_col"] = np.ascontiguousarray(f("glu_b").reshape(8, 128).T)
    m["iota512"] = np.ascontiguousarray(np.tile(np.arange(512, dtype=np.float32)[None, :], (128, 1)))
    m["glu_w"] = f("glu_w")


def phase_S5(c, nc, G, Dr):
    with ExitStack() as es:
        def tb(name, shape, dt):
            return sbt(nc, es, name, shape, dt), Buf(name)
        cb = Buf("s5const")
        def ld(name, shape, dt=F32, eng="sp", src=None):
            t_, _ = tb(name, shape, dt)
            idx = tuple(slice(None) for _ in shape)
            c.dma(eng, t_[idx], src if src is not None else Dr[name][idx], writes=[cb])
            return t_
        lre = ld("lre_b", [128, 512], src=Dr["lre_b"].rearrange("p a b -> p (a b)"))
        lim = ld("lim_b", [128, 512], src=Dr["lim_b"].rearrange("p a b -> p (a b)"))
        lst = ld("lstep_b", [128, 512], src=Dr["lstep_b"].rearrange("p a b -> p (a b)"))
        bre = ld("bT_re", [128, 512], src=Dr["bT_re"].rearrange("p a b -> p (a b)"))
        bim = ld("bT_im", [128, 512], src=Dr["bT_im"].rearrange("p a b -> p (a b)"))
        lreP = ld("lreP", [128, NPAIR]); limP = ld("limP", [128, NPAIR]); lstP = ld("lstepP", [128, NPAIR])
        cre = ld("cpad_re", [128, NPAIR, 128], BF16, eng="pool")
        cim = ld("cpad_im", [128, NPAIR, 128], BF16, eng="pool")
        maskcol = ld("maskcol", [128, 8]); dcol = ld("dcol", [128, 8]); gbcol = ld("glub_col", [128, 8])
        iota = ld("iota512", [128, 512])
        gw = ld("glu_w", [128, 8, 1024], BF16, eng="pool", src=Dr["glu_w"].rearrange("(kt p) n -> p kt n", p=128))
        c.op("dve", lambda e: e.tensor_scalar(out=cim[:, :, :], in0=cim[:, :, :], scalar1=-1.0, scalar2=None, op0=ALU.mult),
             reads=[cb], writes=[cb])
        halfpi, _ = tb("halfpi", [128, 1], F32)
        c.op("dve", lambda e: e.memset(halfpi[:, :], math.pi / 2), writes=[cb])

        W = {}
        def wt(n, shape=(128, 512), dt=F32):
            W[n] = tb("w_" + n, list(shape), dt)
            return W[n][0]
        wb = Buf("s5work")

        def dv(fn):
            c.op("dve", fn, reads=[cb, wb], writes=[wb])

        def ac(fn):
            c.op("act", fn, reads=[cb, wb], writes=[wb])

        def sincos(theta, sin_o, cos_o, q, qi, fr, n):
            dv(lambda e: e.tensor_scalar(out=q[:, :n], in0=theta[:, :n], scalar1=1.0 / TWO_PI, scalar2=None, op0=ALU.mult))
            dv(lambda e: e.tensor_copy(out=qi[:, :n], in_=q[:, :n]))
            dv(lambda e: e.tensor_copy(out=fr[:, :n], in_=qi[:, :n]))
            dv(lambda e: e.tensor_tensor(out=fr[:, :n], in0=q[:, :n], in1=fr[:, :n], op=ALU.subtract))
            ac(lambda e: e.activation(out=sin_o[:, :n], in_=fr[:, :n], func=AF.Sin, scale=TWO_PI))
            dv(lambda e: e.tensor_scalar(out=q[:, :n], in0=fr[:, :n], scalar1=-1.0, scalar2=None, op0=ALU.mult))
            dv(lambda e: e.tensor_tensor(out=fr[:, :n], in0=fr[:, :n], in1=q[:, :n], op=ALU.max))
            ac(lambda e: e.activation(out=cos_o[:, :n], in_=fr[:, :n], func=AF.Sin, scale=-TWO_PI, bias=halfpi[:, 0:1]))

        step = wt("step"); rr_ = wt("r"); th = wt("th"); sn = wt("sn"); cs = wt("cs"); q_ = wt("q"); fr = wt("fr")
        qi = wt("qi", dt=I32); t1 = wt("t1"); t2 = wt("t2"); fre = wt("fre"); fim = wt("fim")
        BbR = wt("BbR"); BbI = wt("BbI")
        ac(lambda e: e.activation(out=step[:, :], in_=lst[:, :], func=AF.Exp))
        dv(lambda e: e.tensor_tensor(out=t1[:, :], in0=lre[:, :], in1=step[:, :], op=ALU.mult))
        ac(lambda e: e.activation(out=rr_[:, :], in_=t1[:, :], func=AF.Exp))
        dv(lambda e: e.tensor_tensor(out=th[:, :], in0=lim[:, :], in1=step[:, :], op=ALU.mult))
        sincos(th, sn, cs, q_, qi, fr, 512)
        dv(lambda e: e.tensor_tensor(out=cs[:, :], in0=cs[:, :], in1=rr_[:, :], op=ALU.mult))
        dv(lambda e: e.tensor_scalar(out=cs[:, :], in0=cs[:, :], scalar1=-1.0, scalar2=None, op0=ALU.add))
        dv(lambda e: e.tensor_tensor(out=sn[:, :], in0=sn[:, :], in1=rr_[:, :], op=ALU.mult))
        dv(lambda e: e.tensor_tensor(out=t1[:, :], in0=lre[:, :], in1=lre[:, :], op=ALU.mult))
        dv(lambda e: e.tensor_tensor(out=t2[:, :], in0=lim[:, :], in1=lim[:, :], op=ALU.mult))
        dv(lambda e: e.tensor_tensor(out=t1[:, :], in0=t1[:, :], in1=t2[:, :], op=ALU.add))
        dv(lambda e: e.reciprocal(out=t1[:, :], in_=t1[:, :]))
        dv(lambda e: e.tensor_tensor(out=fre[:, :], in0=cs[:, :], in1=lre[:, :], op=ALU.mult))
        dv(lambda e: e.tensor_tensor(out=t2[:, :], in0=sn[:, :], in1=lim[:, :], op=ALU.mult))
        dv(lambda e: e.tensor_tensor(out=fre[:, :], in0=fre[:, :], in1=t2[:, :], op=ALU.add))
        dv(lambda e: e.tensor_tensor(out=fre[:, :], in0=fre[:, :], in1=t1[:, :], op=ALU.mult))
        dv(lambda e: e.tensor_tensor(out=fim[:, :], in0=sn[:, :], in1=lre[:, :], op=ALU.mult))
        dv(lambda e: e.tensor_tensor(out=t2[:, :], in0=cs[:, :], in1=lim[:, :], op=ALU.mult))
        dv(lambda e: e.tensor_tensor(out=fim[:, :], in0=fim[:, :], in1=t2[:, :], op=ALU.subtract))
        dv(lambda e: e.tensor_tensor(out=fim[:, :], in0=fim[:, :], in1=t1[:, :], op=ALU.mult))
        dv(lambda e: e.tensor_tensor(out=BbR[:, :], in0=fre[:, :], in1=bre[:, :], op=ALU.mult))
        dv(lambda e: e.tensor_tensor(out=t2[:, :], in0=fim[:, :], in1=bim[:, :], op=ALU.mult))
        dv(lambda e: e.tensor_tensor(out=BbR[:, :], in0=BbR[:, :], in1=t2[:, :], op=ALU.subtract))
        dv(lambda e: e.tensor_tensor(out=BbI[:, :], in0=fre[:, :], in1=bim[:, :], op=ALU.mult))
        dv(lambda e: e.tensor_tensor(out=t2[:, :], in0=fim[:, :], in1=bre[:, :], op=ALU.mult))
        dv(lambda e: e.tensor_tensor(out=BbI[:, :], in0=BbI[:, :], in1=t2[:, :], op=ALU.add))
        lhsB, _ = tb("lhsB", [128, NPAIR, 2, 128], BF16)
        for pr in range(NPAIR):
            gt = pr // 4
            for half in range(2):
                g8 = 2 * (pr % 4) + half
                for ri, src in enumerate((BbR, BbI)):
                    dv(lambda e, pr=pr, half=half, g8=g8, ri=ri, src=src, gt=gt: e.tensor_scalar(
                        out=lhsB[:, pr, ri, half * 64:(half + 1) * 64], in0=src[:, gt * 64:(gt + 1) * 64],
                        scalar1=maskcol[:, g8:g8 + 1], scalar2=None, op0=ALU.mult))
        stP = wt("stP", (128, NPAIR)); rP = wt("rP", (128, NPAIR)); thq = wt("thq", (128, NPAIR))
        psi = wt("psi", (128, 4, NPAIR)); pq = wt("pq", (128, NPAIR)); pqi = wt("pqi", (128, NPAIR), I32)
        ac(lambda e: e.activation(out=stP[:, :], in_=lstP[:, :], func=AF.Exp))
        dv(lambda e: e.tensor_tensor(out=rP[:, :], in0=lreP[:, :], in1=stP[:, :], op=ALU.mult))
        ac(lambda e: e.activation(out=rP[:, :], in_=rP[:, :], func=AF.Exp))
        dv(lambda e: e.tensor_tensor(out=thq[:, :], in0=limP[:, :], in1=stP[:, :], op=ALU.mult))
        dv(lambda e: e.tensor_scalar(out=thq[:, :], in0=thq[:, :], scalar1=1.0 / TWO_PI, scalar2=None, op0=ALU.mult))
        dv(lambda e: e.memset(psi[:, 0, :], 0.0))
        for tc in range(1, 4):
            dv(lambda e, tc=tc: e.tensor_scalar(out=pq[:, :], in0=thq[:, :], scalar1=512.0 * tc, scalar2=None, op0=ALU.mult))
            dv(lambda e: e.tensor_copy(out=pqi[:, :], in_=pq[:, :]))
            dv(lambda e, tc=tc: e.tensor_copy(out=psi[:, tc, :], in_=pqi[:, :]))
            dv(lambda e, tc=tc: e.tensor_tensor(out=psi[:, tc, :], in0=pq[:, :], in1=psi[:, tc, :], op=ALU.subtract))

        uT = [tb(f"uT{i}", [128, T], BF16) for i in range(2)]
        hT, _ = tb("hT", [128, 8, T], BF16)
        hT_b = [[Buf() for _ in range(4)] for _ in range(8)]
        NB = 2
        Q = [tb(f"Q{i}", [128, 512], F32) for i in range(NB)]
        QI = [tb(f"QI{i}", [128, 512], I32) for i in range(NB)]
        FR = [tb(f"FR{i}", [128, 512], F32) for i in range(NB)]
        AB = [tb(f"AB{i}", [128, 512], F32) for i in range(NB)]
        SN = [tb(f"SN{i}", [128, 512], F32) for i in range(NB)]
        CS = [tb(f"CS{i}", [128, 512], F32) for i in range(NB)]
        TA = [tb(f"TA{i}", [128, 512], F32) for i in range(NB)]
        TB_ = [tb(f"TB{i}", [128, 512], F32) for i in range(NB)]
        TC = [tb(f"TC{i}", [128, 512], F32) for i in range(NB)]
        TD = [tb(f"TD{i}", [128, 512], F32) for i in range(NB)]
        XR = [tb(f"XR{i}", [128, 512], BF16) for i in range(NB)]
        XI = [tb(f"XI{i}", [128, 512], BF16) for i in range(NB)]
        ytmp = [tb(f"ytmp{i}", [128, 512], F32) for i in range(2)]
        PY = [G.psum[i] for i in range(4)]
        PB = [(G.psum[4], G.psum[5]), (G.psum[6], G.psum[7])]
        ZRc = [[tb(f"ZRc{i}_{j}", [128, 512], F32) for j in range(2)] for i in range(2)]
        ZIc = [[tb(f"ZIc{i}_{j}", [128, 512], F32) for j in range(2)] for i in range(2)]

        def run_rr(gens):
            gens = list(gens)
            while gens:
                for g_ in list(gens):
                    try:
                        next(g_)
                    except StopIteration:
                        gens.remove(g_)

        def chain(gt, ci, u_, ub_):
            i = ci
            for pl in (2 * ci, 2 * ci + 1):
                pr = gt * 4 + pl
                for tc in range(4):
                    (pbr, pbrb), (pbi, pbib) = PB[ci]
                    sl = slice(tc * 512, (tc + 1) * 512)
                    c.op("pe", lambda e: e.matmul(pbr[:, :], lhsT=lhsB[:, pr, 0, :], rhs=u_[:, sl], start=True, stop=True),
                         reads=[wb, ub_], writes=[pbrb])
                    yield
                    c.op("pe", lambda e: e.matmul(pbi[:, :], lhsT=lhsB[:, pr, 1, :], rhs=u_[:, sl], start=True, stop=True),
                         reads=[wb, ub_], writes=[pbib])
                    yield
                    q, qb = Q[i]; qi_, qib = QI[i]; fr_, frb = FR[i]; ab, abb = AB[i]; s_, sb_ = SN[i]; c_, cb_ = CS[i]
                    ta, tab = TA[i]; tb2, tbb = TB_[i]; tcc, tcb = TC[i]; td, tdb = TD[i]
                    zr, zrb = ZRc[ci][tc % 2]; zi, zib = ZIc[ci][tc % 2]; xr, xrb = XR[i]; xi, xib = XI[i]
                    c.op("act", lambda e: e.activation(out=q[:, :], in_=iota[:, :], func=AF.Identity, scale=thq[:, pr:pr + 1],
                                                       bias=psi[:, tc, pr:pr + 1]), reads=[cb, wb], writes=[qb])
                    yield
                    c.op("dve", lambda e: e.tensor_copy(out=qi_[:, :], in_=q[:, :]), reads=[qb], writes=[qib])
                    yield
                    c.op("dve", lambda e: e.tensor_copy(out=fr_[:, :], in_=qi_[:, :]), reads=[qib], writes=[frb])
                    yield
                    c.op("dve", lambda e: e.tensor_tensor(out=fr_[:, :], in0=q[:, :], in1=fr_[:, :], op=ALU.subtract),
                         reads=[qb, frb], writes=[frb])
                    yield
                    c.op("act", lambda e: e.activation(out=s_[:, :], in_=fr_[:, :], func=AF.Sin, scale=TWO_PI),
                         reads=[frb], writes=[sb_])
                    yield
                    c.op("act", lambda e: e.activation(out=ab[:, :], in_=fr_[:, :], func=AF.Sin, scale=math.pi),
                         reads=[frb], writes=[abb])
                    yield
                    c.op("act", lambda e: e.activation(out=ab[:, :], in_=ab[:, :], func=AF.Square),
                         reads=[abb], writes=[abb])
                    yield
                    c.op("act", lambda e: e.activation(out=c_[:, :], in_=ab[:, :], func=AF.Copy, scale=-2.0, bias=1.0),
                         reads=[abb], writes=[cb_])
                    yield
                    c.op("dve", lambda e: e.tensor_tensor(out=ta[:, :], in0=c_[:, :], in1=pbr[:, :], op=ALU.mult),
                         reads=[cb_, pbrb], writes=[tab])
                    yield
                    c.op("dve", lambda e: e.tensor_tensor(out=tb2[:, :], in0=s_[:, :], in1=pbi[:, :], op=ALU.mult),
                         reads=[sb_, pbib], writes=[tbb])
                    yield
                    c.op("dve", lambda e: e.tensor_tensor(out=tcc[:, :], in0=c_[:, :], in1=pbi[:, :], op=ALU.mult),
                         reads=[cb_, pbib], writes=[tcb])
                    yield
                    c.op("dve", lambda e: e.tensor_tensor(out=td[:, :], in0=s_[:, :], in1=pbr[:, :], op=ALU.mult),
                         reads=[sb_, pbrb], writes=[tdb])
                    yield
                    c.op("pool", lambda e: e.tensor_tensor(out=ta[:, :], in0=ta[:, :], in1=tb2[:, :], op=ALU.add),
                         reads=[tab, tbb], writes=[tab])
                    yield
                    c.op("pool", lambda e: e.tensor_tensor(out=tcc[:, :], in0=tcc[:, :], in1=td[:, :], op=ALU.subtract),
                         reads=[tcb, tdb], writes=[tcb])
                    yield
                    zrp, zrpb = ZRc[ci][(tc + 1) % 2]; zip_, zipb = ZIc[ci][(tc + 1) % 2]
                    init_r = 0.0 if tc == 0 else zrp[:, 511:512]
                    init_i = 0.0 if tc == 0 else zip_[:, 511:512]
                    rbc = rP[:, pr:pr + 1].to_broadcast([128, 512])
                    c.op("dve", lambda e: e.tensor_tensor_scan(out=zr[:, :], data0=rbc, data1=ta[:, :], initial=init_r,
                                                               op0=ALU.mult, op1=ALU.add),
                         reads=[tab, wb] + ([zrpb] if tc else []), writes=[zrb])
                    yield
                    c.op("dve", lambda e: e.tensor_tensor_scan(out=zi[:, :], data0=rbc, data1=tcc[:, :], initial=init_i,
                                                               op0=ALU.mult, op1=ALU.add),
                         reads=[tcb, wb] + ([zipb] if tc else []), writes=[zib])
                    yield
                    c.op("dve", lambda e: e.tensor_tensor(out=ta[:, :], in0=c_[:, :], in1=zr[:, :], op=ALU.mult),
                         reads=[cb_, zrb], writes=[tab])
                    yield
                    c.op("dve", lambda e: e.tensor_tensor(out=tb2[:, :], in0=s_[:, :], in1=zi[:, :], op=ALU.mult),
                         reads=[sb_, zib], writes=[tbb])
                    yield
                    c.op("pool", lambda e: e.tensor_tensor(out=xr[:, :], in0=ta[:, :], in1=tb2[:, :], op=ALU.subtract),
                         reads=[tab, tbb], writes=[xrb])
                    yield
                    c.op("dve", lambda e: e.tensor_tensor(out=tcc[:, :], in0=s_[:, :], in1=zr[:, :], op=ALU.mult),
                         reads=[sb_, zrb], writes=[tcb])
                    yield
                    c.op("dve", lambda e: e.tensor_tensor(out=td[:, :], in0=c_[:, :], in1=zi[:, :], op=ALU.mult),
                         reads=[cb_, zib], writes=[tdb])
                    yield
                    c.op("pool", lambda e: e.tensor_tensor(out=xi[:, :], in0=tcc[:, :], in1=td[:, :], op=ALU.add),
                         reads=[tcb, tdb], writes=[xib])
                    yield
                    py, pyb = PY[tc]
                    c.op("pe", lambda e: e.matmul(py[:, :], lhsT=cre[:, pr, :], rhs=xr[:, :], start=(pl == 0), stop=False),
                         reads=[cb, xrb], writes=[pyb])
                    yield
                    c.op("pe", lambda e: e.matmul(py[:, :], lhsT=cim[:, pr, :], rhs=xi[:, :], start=False, stop=(pl == 3)),
                         reads=[cb, xib], writes=[pyb])
                    yield

        for gt in range(8):
            u_, ub_ = uT[gt % 2]
            c.dma("sp", u_[:, :], Dr["projT"][2048 + gt * 128:2048 + (gt + 1) * 128, :], writes=[ub_])
            ga, gb_ = chain(gt, 0, u_, ub_), chain(gt, 1, u_, ub_)
            for _ in range(STAG):
                next(ga)
            run_rr([ga, gb_])
            for tc in range(4):
                py, pyb = PY[tc]
                yt, ytb = ytmp[tc % 2]
                sl = slice(tc * 512, (tc + 1) * 512)
                c.op("dve", lambda e: e.scalar_tensor_tensor(out=yt[:, :], in0=u_[:, sl], scalar=dcol[:, gt:gt + 1],
                                                             in1=py[:, :], op0=ALU.mult, op1=ALU.add),
                     reads=[ub_, cb, pyb], writes=[ytb])
                c.op("act", lambda e: e.activation(out=hT[:, gt, sl], in_=yt[:, :], func=AF.Gelu_apprx_tanh),
                     reads=[ytb], writes=[hT_b[gt][tc]])
        sg = [tb(f"sg{i}", [128, 512], F32) for i in range(2)]
        so = [tb(f"so{i}", [128, 512], BF16) for i in range(2)]
        k = 0
        for nt in range(8):
            for tc in range(4):
                pz, pzb = G.psum[k % 8]
                sl = slice(tc * 512, (tc + 1) * 512)
                for kt in range(8):
                    c.op("pe", lambda e, kt=kt: e.matmul(pz[:, :], lhsT=gw[:, kt, nt * 128:(nt + 1) * 128], rhs=hT[:, kt, sl],
                                                        start=(kt == 0), stop=(kt == 7)),
                         reads=[cb, hT_b[kt][tc]], writes=[pzb])
                sg_, sgb = sg[k % 2]; so_, sob = so[k % 2]
                c.op("act", lambda e: e.activation(out=sg_[:, :], in_=pz[:, :], func=AF.Sigmoid, bias=gbcol[:, nt:nt + 1]),
                     reads=[pzb, cb], writes=[sgb])
                c.op("dve", lambda e: e.tensor_tensor(out=so_[:, :], in0=sg_[:, :], in1=hT[:, nt, sl], op=ALU.mult),
                     reads=[sgb, hT_b[nt][tc]], writes=[sob])
                c.dma("sp", Dr["mixT"][1024 + nt * 128:1024 + (nt + 1) * 128, sl], so_[:, :], reads=[sob])
                k += 1
        c.barrier()


def extra_inputs(din):
    nsa_inputs(din); s5_inputs(din)
def extra_scratch(dscr):
    nsa_scratch(dscr)
def extra_host(m, inp):
    nsa_host(m, inp); s5_host(m, inp)

DEBUG_OUT = []
PHASES = ("A", "NSA", "S5", "D")
LAST_RES = None


def build_nc(needed=None, dry=False):
    nc = bass.Bass("TRN2", target_bir_lowering=False)
    Dr = {}

    def din(name, shape, dt=F32):
        Dr[name] = nc.dram_tensor(name, list(shape), dt, kind="ExternalInput").ap()

    def dscr(name, shape, dt):
        kind = "ExternalOutput" if name in DEBUG_OUT else "Internal"
        Dr[name] = nc.dram_tensor(name, list(shape), dt, kind=kind).ap()

    if "A" in PHASES or "D" in PHASES:
        din("x", [T, D_MODEL])
        for p in ("ffn1", "ffn2"):
            din(p + "_w1", [D_MODEL, D_FF]); din(p + "_w3", [D_MODEL, D_FF]); din(p + "_w2", [D_FF, D_MODEL])
        din("w_in_fm", [D_MODEL, 3200]); din("w_in_v", [D_MODEL, 512]); din("w_out", [D_MODEL, D_MODEL])
    din("gains_l", [128, 64]); din("ident", [128, 128])
    extra_inputs(din)
    Dr["out"] = nc.dram_tensor("out", [T, D_MODEL], F32, kind="ExternalOutput").ap()
    dscr("h1T", [D_MODEL, T], F32)
    if "A" in PHASES:
        dscr("projT", [3072, T], BF16)
        dscr("gT", [128, T], F32)
        dscr("vtm", [T, 512], BF16)
    else:
        din("projT", [3072, T], BF16); din("gT", [128, T]); din("vtm", [T, 512], BF16)
    if "A" in PHASES and "NSA" not in PHASES and "mixT_in" in DEBUG_OUT:
        Dr["mixT"] = nc.dram_tensor("mixT", [D_MODEL, T], BF16, kind="ExternalInput").ap()
    else:
        dscr("mixT", [D_MODEL, T], BF16)
    extra_scratch(dscr)
    for p in ("c1", "c2"):
        dscr(p + "_w1", [11, 128, KT, 512], BF16); dscr(p + "_w3", [11, 128, KT, 512], BF16)
        dscr(p + "_w2", [KT, 128, FT, 128], BF16)
    dscr("c_win", [6, 128, KT, 512], BF16); dscr("c_wing", [128, KT, 128], BF16); dscr("c_winv", [128, KT, 512], BF16)
    dscr("c_wout", [4, 128, KT, 512], BF16)

    with ExitStack() as es:
        c = Ctx(nc, es, needed)
        G = types.SimpleNamespace()
        G.psum = [(es.enter_context(nc.psum_tensor(f"ps{i}", [128, 512], F32)), Buf()) for i in range(8)]
        G.ones_bf = sbt(nc, es, "ones_bf", [128, 128], BF16)
        G.ident32 = sbt(nc, es, "ident32", [128, 128], F32)
        G.gains = sbt(nc, es, "gains", [128, 64], F32)
        G.const_b = Buf()
        c.op("dve", lambda e: e.memset(G.ones_bf[:, :], 1.0), writes=[G.const_b])
        c.dma("sp", G.ident32[:, :], Dr["ident"][:, :], writes=[G.const_b])
        c.dma("sp", G.gains[:, :], Dr["gains_l"][:, :], writes=[G.const_b])
        if "A" in PHASES:
            phase_A(c, nc, G, Dr)
        if "NSA" in PHASES:
            phase_NSA(c, nc, G, Dr)
        if "S5" in PHASES:
            phase_S5(c, nc, G, Dr)
        if "D" in PHASES:
            phase_D(c, nc, G, Dr)
        c.final_wait("sp")
        if dry:
            return set(c.rec)
    return nc


def host_inputs(inp, b):
    sq = lambda a: np.ascontiguousarray(np.asarray(a)[0])
    w_in = sq(inp["w_in"])
    q = w_in[:, 0:1024]; kc = w_in[:, 1024:1280]; vc = w_in[:, 1280:1536]; ks = w_in[:, 1536:1792]
    vs = w_in[:, 1792:2048]; kw = w_in[:, 2048:2304]; vw = w_in[:, 2304:2560]; gt = w_in[:, 2560:2584]
    u = w_in[:, 2584:3608]
    gpad = np.zeros((D_MODEL, 128), np.float32); gpad[:, :24] = gt
    m = {
        "x": np.ascontiguousarray(np.asarray(inp["x"])[b]),
        "w_in_fm": np.ascontiguousarray(np.concatenate([q, kc, vc, ks, kw, u, gpad], axis=1)),
        "w_in_v": np.ascontiguousarray(np.concatenate([vs, vw], axis=1)),
        "w_out": sq(inp["w_out"]),
        "ident": np.eye(128, dtype=np.float32),
    }
    for p in ("ffn1", "ffn2"):
        for w in ("w1", "w3", "w2"):
            m[f"{p}_{w}"] = sq(inp[f"{p}_{w}"])
    gl = np.zeros((128, 64), np.float32)
    for gi, g in enumerate((sq(inp["ffn1_norm"]), sq(inp["mix_norm"]), sq(inp["ffn2_norm"]), np.asarray(inp["final_norm"]))):
        gl[:, gi * 16:(gi + 1) * 16] = g.reshape(16, 128).T
    m["gains_l"] = gl
    extra_host(m, inp)
    return m


_NC_CACHE = {}


def kernel(**inputs):
    global LAST_RES
    key = (tuple(DEBUG_OUT), tuple(PHASES))
    if key not in _NC_CACHE:
        _NC_CACHE[key] = build_nc(needed=build_nc(dry=True))
    nc = _NC_CACHE[key]
    shared = host_inputs(inputs, 0)
    in_maps = []
    for b in range(8):
        m = dict(shared)
        m["x"] = np.ascontiguousarray(np.asarray(inputs["x"])[b])
        if "mixT_in" in DEBUG_OUT:
            import ml_dtypes
            m["mixT"] = np.zeros((D_MODEL, T), ml_dtypes.bfloat16)
        in_maps.append(m)
    res = run_bass_kernel_spmd(nc, in_maps, core_ids=list(range(8)))
    LAST_RES = res
    return np.stack([np.asarray(r["out"]) for r in res.results], axis=0).astype(np.float32)
```

```python
import types
from contextlib import ExitStack
from concourse.bass_utils import run_bass_kernel_spmd

import numpy as np
import concourse.bass as bass
import concourse.mybir as mybir

F32 = mybir.dt.float32
BF16 = mybir.dt.bfloat16
I32 = mybir.dt.int32
AF = mybir.ActivationFunctionType
ALU = mybir.AluOpType
AX = mybir.AxisListType

CHUNK = 16000
NDMASEM = 96


class Buf:
    __slots__ = ("name", "lw", "rd")

    def __init__(self, name=""):
        self.name = name
        self.lw = None
        self.rd = []


class Ctx:
    def __init__(self, nc, es, needed=None):
        self.needed = needed
        self.rec = set()
        self.nc = nc
        self.es = es
        self.eng = {"pe": nc.tensor, "act": nc.scalar, "dve": nc.vector, "pool": nc.gpsimd, "sp": nc.sync}
        self.count = {e: 0 for e in self.eng}
        self.sems = {e: [] for e in self.eng}
        self.waited = {e: {} for e in self.eng}
        self.dsem = [es.enter_context(nc.semaphore(f"dq{i}")) for i in range(NDMASEM)]
        self.dval = [0] * NDMASEM
        self.dlast = [None] * NDMASEM
        self.dnext = {e: 0 for e in self.eng}
        self.n_instr = 0
        self.sig = {}
        self.nsig = {e: 0 for e in self.eng}
        self.drained = {e: 0 for e in self.eng}

    def _sem_for(self, e, idx):
        k = (idx - 1) // CHUNK
        while len(self.sems[e]) <= k:
            self.sems[e].append(self.es.enter_context(self.nc.semaphore(f"s_{e}_{len(self.sems[e])}")))
        return self.sems[e][k], (idx - 1) % CHUNK + 1

    def _wait(self, e, tok, raw=True):
        if tok is None:
            return
        if tok[0] == "dma":
            _, k, val = tok
            key = ("dma", k)
            if self.waited[e].get(key, 0) >= val:
                return
            self.eng[e].wait_ge(self.dsem[k], val)
            self.waited[e][key] = val
        else:
            e2, idx = tok
            if e2 == e and (e in ("pe", "pool") or not raw):
                return
            if self.waited[e].get(e2, 0) >= idx:
                return
            self.rec.add((e2, idx))
            if self.needed is None:
                sem, val = self._sem_for(e2, idx)
            else:
                sem, val = self._sem_for(e2, self.sig[(e2, idx)])
            self.eng[e].wait_ge(sem, val)
            self.waited[e][e2] = idx
        self.n_instr += 1

    def _deps(self, e, reads, writes):
        for b in reads:
            self._wait(e, b.lw, True)
        for b in writes:
            self._wait(e, b.lw, False)
            for t in b.rd:
                self._wait(e, t, False)

    def _commit(self, tok, reads, writes):
        for b in reads:
            b.rd.append(tok)
        for b in writes:
            b.lw = tok
            b.rd = []

    def op(self, e, fn, reads=(), writes=()):
        self._deps(e, reads, writes)
        ins = fn(self.eng[e])
        self.count[e] += 1
        idx = self.count[e]
        if self.needed is None:
            sem, _ = self._sem_for(e, idx)
            ins.then_inc(sem, 1)
        elif (e, idx) in self.needed:
            self.nsig[e] += 1
            self.sig[(e, idx)] = self.nsig[e]
            sem, _ = self._sem_for(e, self.nsig[e])
            ins.then_inc(sem, 1)
        self._commit((e, idx), reads, writes)
        self.n_instr += 1
        return ins

    def dma(self, e, out, in_, reads=(), writes=(), **kw):
        half = NDMASEM // 2
        base = half if e == "pool" else 0
        k = base + self.dnext[e]
        self.dnext[e] = (self.dnext[e] + 1) % half
        self._wait(e, self.dlast[k])
        self._deps(e, reads, writes)
        ins = self.eng[e].dma_start(out=out, in_=in_, **kw)
        ins.then_inc(self.dsem[k], 16)
        self.dval[k] += 16
        tok = ("dma", k, self.dval[k])
        self.dlast[k] = tok
        self._commit(tok, reads, writes)
        self.n_instr += 1
        return ins

    def barrier(self):
        for e in self.eng:
            for e2 in self.eng:
                if e2 != e and self.count[e2] > 0:
                    self._wait(e, (e2, self.count[e2]))
            for k in range(NDMASEM):
                self._wait(e, self.dlast[k])

    def final_wait(self, e="sp"):
        for k in range(NDMASEM):
            self._wait(e, self.dlast[k])
        for e2 in self.eng:
            if e2 != e and self.count[e2] > 0:
                self._wait(e, (e2, self.count[e2]))

from contextlib import ExitStack

D_MODEL = 2048; T = 2048; D_FF = 5632; KT = 16; FT = 44
TT = 512
NSUB = TT // 128
EPS = 1e-6


_SBT_N = [0]


def sbt(nc, es, name, shape, dtype):
    _SBT_N[0] += 1
    return es.enter_context(nc.sbuf_tensor(f"{name}_{_SBT_N[0]}", list(shape), dtype))


class WStream:
    def __init__(self, c, pools, items, depth=1, eng="pool"):
        self.c = c; self.pools = pools; self.items = items; self.depth = depth; self.eng = eng
        self.issued = 0
        self.pcount = {p: 0 for p in pools}
        self.loc = {}

    def _issue(self, i):
        it = self.items[i]
        pool, src, dst_sl = it[0], it[1], it[2]
        cache, fill = (it[3], it[4]) if len(it) > 3 else (None, False)
        tens, bufs = self.pools[pool]
        j = self.pcount[pool] % len(tens)
        self.pcount[pool] += 1
        dst = tens[j][:] if dst_sl is None else dst_sl(tens[j])
        if cache is None or fill:
            self.c.dma(self.eng, dst, src, writes=[bufs[j]])
            if cache is not None:
                self.c.dma("sp", cache, dst, reads=[bufs[j]])
        else:
            self.c.dma(self.eng, dst, cache, writes=[bufs[j]])
        self.loc[i] = (tens[j], bufs[j])

    def get(self, i):
        while self.issued < min(len(self.items), i + 1 + self.depth):
            self._issue(self.issued)
            self.issued += 1
        return self.loc.pop(i)


def rmsnorm_fm(c, S, gain_t, gk, out32=False):
    nc = c.nc
    psn, psn_b = S.ps["norm"]
    for dk in range(KT):
        c.op("act", lambda e, dk=dk: e.activation(out=S.sq[:, dk, :], in_=S.xT[:, dk, :], func=AF.Square),
             reads=[S.xT_b[dk]], writes=[S.sq_b[dk]])
    for dk in range(KT):
        c.op("pe", lambda e, dk=dk: e.matmul(psn[:, :TT], lhsT=S.ones_bf[:, :], rhs=S.sq[:, dk, :],
                                            start=(dk == 0), stop=(dk == KT - 1)),
             reads=[S.sq_b[dk], S.const_b], writes=[psn_b])
    c.op("dve", lambda e: e.tensor_scalar(out=S.rstd[:, :], in0=psn[:, :TT], scalar1=1.0 / D_MODEL, scalar2=EPS,
                                          op0=ALU.mult, op1=ALU.add), reads=[psn_b], writes=[S.rstd_b])
    c.op("act", lambda e: e.activation(out=S.rstd[:, :], in_=S.rstd[:, :], func=AF.Sqrt),
         reads=[S.rstd_b], writes=[S.rstd_b])
    c.op("dve", lambda e: e.reciprocal(out=S.rstd[:, :], in_=S.rstd[:, :]), reads=[S.rstd_b], writes=[S.rstd_b])
    ot, ob = (S.xT, S.xT_b) if out32 else (S.xn, S.xn_b)
    for dk in range(KT):
        c.op("dve", lambda e, dk=dk: e.scalar_tensor_tensor(
            out=ot[:, dk, :], in0=S.xT[:, dk, :], scalar=gain_t[:, gk * KT + dk:gk * KT + dk + 1],
            in1=S.rstd[:, :], op0=ALU.mult, op1=ALU.mult),
            reads=[S.xT_b[dk], S.rstd_b, S.const_b], writes=[ob[dk]])


def ffn_items(w1, w3, w2, cache=None, fill=False):
    items = []
    w1v = w1.rearrange("(kt p) f -> p kt f", p=128)
    w3v = w3.rearrange("(kt p) f -> p kt f", p=128)
    w2v = w2.rearrange("(ft p) d -> p ft d", p=128)
    for ci in range(FT // 4):
        items.append(("wa", w1v[:, :, ci * 512:(ci + 1) * 512], None) + ((cache[0][ci], fill) if cache else ()))
        items.append(("wb", w3v[:, :, ci * 512:(ci + 1) * 512], None) + ((cache[1][ci], fill) if cache else ()))
    for dt in range(KT):
        items.append(("wc", w2v[:, :, dt * 128:(dt + 1) * 128], None) + ((cache[2][dt], fill) if cache else ()))
    return items


def ffn_fm(c, S, ws, base):
    g1 = [S.ps["g1a"], S.ps["g1b"]]; g3 = [S.ps["g3a"], S.ps["g3b"]]
    for ci in range(FT // 4):
        wa, wa_b = ws.get(base + 2 * ci)
        wb, wb_b = ws.get(base + 2 * ci + 1)
        for j in range(4):
            ft = 4 * ci + j
            p1, p1b = g1[ft % 2]; p3, p3b = g3[ft % 2]
            for kt in range(KT):
                c.op("pe", lambda e, kt=kt, wa=wa, j=j, p1=p1: e.matmul(
                    p1[:, :TT], lhsT=wa[:, kt, j * 128:(j + 1) * 128], rhs=S.xn[:, kt, :],
                    start=(kt == 0), stop=(kt == KT - 1)), reads=[wa_b, S.xn_b[kt]], writes=[p1b])
            for kt in range(KT):
                c.op("pe", lambda e, kt=kt, wb=wb, j=j, p3=p3: e.matmul(
                    p3[:, :TT], lhsT=wb[:, kt, j * 128:(j + 1) * 128], rhs=S.xn[:, kt, :],
                    start=(kt == 0), stop=(kt == KT - 1)), reads=[wb_b, S.xn_b[kt]], writes=[p3b])
            st, stb = S.silu[ft % 2]
            c.op("act", lambda e, st=st, p1=p1: e.activation(out=st[:, :], in_=p1[:, :TT], func=AF.Silu),
                 reads=[p1b], writes=[stb])
            c.op("dve", lambda e, st=st, p3=p3, ft=ft: e.tensor_tensor(
                out=S.act[:, ft, :], in0=st[:, :], in1=p3[:, :TT], op=ALU.mult),
                reads=[stb, p3b], writes=[S.act_b[ft]])
    base += FT // 2
    yp = [S.ps["ya"], S.ps["yb"]]
    for dt in range(KT):
        wc, wc_b = ws.get(base + dt)
        py, pyb = yp[dt % 2]
        for ft in range(FT):
            c.op("pe", lambda e, ft=ft, wc=wc, py=py: e.matmul(
                py[:, :TT], lhsT=wc[:, ft, :], rhs=S.act[:, ft, :], start=(ft == 0), stop=(ft == FT - 1)),
                reads=[wc_b, S.act_b[ft]], writes=[pyb])
        c.op("dve", lambda e, dt=dt, py=py: e.scalar_tensor_tensor(
            out=S.xT[:, dt, :], in0=py[:, :TT], scalar=0.5, in1=S.xT[:, dt, :], op0=ALU.mult, op1=ALU.add),
            reads=[pyb, S.xT_b[dt]], writes=[S.xT_b[dt]])
    return base + KT

import types

PSN = ["g1a", "g1b", "g3a", "g3b", "ya", "yb", "norm", "tr"]
NT_FM = 25


def alloc_AD(c, nc, es, G):
    S = types.SimpleNamespace()
    S.ps = {n: G.psum[i] for i, n in enumerate(PSN)}
    S.xT = sbt(nc, es, "xT", [128, KT, TT], F32); S.xT_b = [Buf() for _ in range(KT)]
    S.xn = sbt(nc, es, "xn", [128, KT, TT], BF16); S.xn_b = [Buf() for _ in range(KT)]
    S.act = sbt(nc, es, "act", [128, FT, TT], BF16); S.act_b = [Buf() for _ in range(FT)]
    S.sq = S.act; S.sq_b = S.act_b
    S.rstd = sbt(nc, es, "rstd", [128, TT], F32); S.rstd_b = Buf()
    S.silu = [(sbt(nc, es, f"silu{i}", [128, TT], F32), Buf()) for i in range(2)]
    S.xs = sbt(nc, es, "xs", [128, D_MODEL], F32); S.xs_b = Buf()
    S.stg = [(sbt(nc, es, f"stg{i}", [128, 512], BF16), Buf()) for i in range(4)]
    S.stg32 = (sbt(nc, es, "stg32", [128, 512], F32), Buf())
    S.pools = {
        "wa": ([sbt(nc, es, f"wa{i}", [128, KT, 512], BF16) for i in range(2)], [Buf() for _ in range(2)]),
        "wb": ([sbt(nc, es, f"wb{i}", [128, KT, 512], BF16) for i in range(2)], [Buf() for _ in range(2)]),
        "wc": ([sbt(nc, es, f"wc{i}", [128, FT, 128], BF16) for i in range(2)], [Buf() for _ in range(2)]),
    }
    S.ones_bf = G.ones_bf; S.ident32 = G.ident32; S.gains = G.gains; S.const_b = G.const_b
    return S


def phase_A(c, nc, G, Dr):
    with ExitStack() as es:
        S = alloc_AD(c, nc, es, G)
        items = []
        wfm = Dr["w_in_fm"].rearrange("(kt p) n -> p kt n", p=128)
        wv = Dr["w_in_v"].rearrange("(kt p) n -> p kt n", p=128)
        for tt in range(T // TT):
            fill = (tt == 0)
            items += ffn_items(Dr["ffn1_w1"], Dr["ffn1_w3"], Dr["ffn1_w2"],
                               (Dr["c1_w1"], Dr["c1_w3"], Dr["c1_w2"]), fill)
            for ci in range(6):
                items.append(("wa", wfm[:, :, ci * 512:(ci + 1) * 512], None, Dr["c_win"][ci], fill))
            items.append(("wa", wfm[:, :, 3072:3200], lambda t: t[:, :, 0:128], Dr["c_wing"][:, :, :], fill))
            items.append(("wb", wv[:, :, :], None, Dr["c_winv"][:, :, :], fill))
        ws = WStream(c, S.pools, items)
        base = 0
        ptr, ptr_b = S.ps["tr"]
        h1v = Dr["h1T"].rearrange("(k p) t -> p k t", p=128)
        rot = [S.ps[n] for n in ("g1a", "g1b", "g3a", "g3b")]
        ev = 0
        for tt in range(T // TT):
            t0 = tt * TT
            for s in range(NSUB):
                c.dma("sp", S.xs[:, :], Dr["x"][t0 + s * 128:t0 + (s + 1) * 128, :], writes=[S.xs_b])
                for i in range(4):
                    for j in range(4):
                        dk = 4 * i + j
                        c.op("pe", lambda e, dk=dk, j=j: e.transpose(
                            out=ptr[:, j * 128:(j + 1) * 128], in_=S.xs[:, dk * 128:(dk + 1) * 128],
                            identity=S.ident32[:, :]), reads=[S.xs_b, S.const_b], writes=[ptr_b])
                    c.op("dve", lambda e, i=i, s=s: e.tensor_copy(
                        out=S.xT[:, 4 * i:4 * i + 4, s * 128:(s + 1) * 128],
                        in_=ptr[:, :].rearrange("p (j t) -> p j t", j=4)),
                        reads=[ptr_b], writes=S.xT_b[4 * i:4 * i + 4])
            rmsnorm_fm(c, S, S.gains, 0)
            base = ffn_fm(c, S, ws, base)
            c.dma("sp", h1v[:, :, t0:t0 + TT], S.xT[:, :, :], reads=S.xT_b)
            rmsnorm_fm(c, S, S.gains, 1)
            for ci in range(7):
                wa, wa_b = ws.get(base + ci)
                for j in range(4 if ci < 6 else 1):
                    nt = 4 * ci + j
                    pp, ppb = rot[nt % 4]
                    for kt in range(KT):
                        c.op("pe", lambda e, kt=kt, wa=wa, j=j, pp=pp: e.matmul(
                            pp[:, :TT], lhsT=wa[:, kt, j * 128:(j + 1) * 128], rhs=S.xn[:, kt, :],
                            start=(kt == 0), stop=(kt == KT - 1)), reads=[wa_b, S.xn_b[kt]], writes=[ppb])
                    eng = "act" if ev % 2 == 0 else "dve"
                    if nt < 24:
                        st, stb = S.stg[ev % 4]
                        dst = Dr["projT"][nt * 128:(nt + 1) * 128, t0:t0 + TT]
                    else:
                        st, stb = S.stg32
                        dst = Dr["gT"][:, t0:t0 + TT]
                    if eng == "act":
                        c.op("act", lambda e, st=st, pp=pp: e.copy(out=st[:, :TT], in_=pp[:, :TT]),
                             reads=[ppb], writes=[stb])
                    else:
                        c.op("dve", lambda e, st=st, pp=pp: e.tensor_copy(out=st[:, :TT], in_=pp[:, :TT]),
                             reads=[ppb], writes=[stb])
                    ev += 1
                    c.dma("sp", dst, st[:, :TT], reads=[stb])
            base += 7
            wv0, wv0_b = ws.get(base)
            base += 1
            for s in range(NSUB):
                pp, ppb = rot[s % 4]
                for kt in range(KT):
                    c.op("pe", lambda e, kt=kt, s=s, pp=pp: e.matmul(
                        pp[:, :], lhsT=S.xn[:, kt, s * 128:(s + 1) * 128],
                        rhs=wv0[:, kt, :], start=(kt == 0), stop=(kt == KT - 1)),
                        reads=[wv0_b, S.xn_b[kt]], writes=[ppb])
                st, stb = S.stg[ev % 4]; ev += 1
                c.op("dve", lambda e, st=st, pp=pp: e.tensor_copy(out=st[:, :], in_=pp[:, :]),
                     reads=[ppb], writes=[stb])
                c.dma("sp", Dr["vtm"][t0 + s * 128:t0 + (s + 1) * 128, :], st[:, :], reads=[stb])
        c.barrier()


def phase_D(c, nc, G, Dr):
    with ExitStack() as es:
        S = alloc_AD(c, nc, es, G)
        items = []
        wo = Dr["w_out"].rearrange("(kt p) n -> p kt n", p=128)
        for tt in range(T // TT):
            fill = (tt == 0)
            for ci in range(4):
                items.append(("wa", wo[:, :, ci * 512:(ci + 1) * 512], None, Dr["c_wout"][ci], fill))
            items += ffn_items(Dr["ffn2_w1"], Dr["ffn2_w3"], Dr["ffn2_w2"],
                               (Dr["c2_w1"], Dr["c2_w3"], Dr["c2_w2"]), fill)
        ws = WStream(c, S.pools, items)
        base = 0
        ptr, ptr_b = S.ps["tr"]
        h1v = Dr["h1T"].rearrange("(k p) t -> p k t", p=128)
        mixv = Dr["mixT"].rearrange("(k p) t -> p k t", p=128)
        rot = [S.ps[n] for n in ("g1a", "g1b", "g3a", "g3b")]
        for tt in range(T // TT):
            t0 = tt * TT
            c.dma("sp", S.xT[:, :, :], h1v[:, :, t0:t0 + TT], writes=S.xT_b)
            c.dma("sp", S.xn[:, :, :], mixv[:, :, t0:t0 + TT], writes=S.xn_b)
            for ci in range(4):
                wa, wa_b = ws.get(base + ci)
                for j in range(4):
                    dt = 4 * ci + j
                    pp, ppb = rot[dt % 4]
                    for kt in range(KT):
                        c.op("pe", lambda e, kt=kt, wa=wa, j=j, pp=pp: e.matmul(
                            pp[:, :TT], lhsT=wa[:, kt, j * 128:(j + 1) * 128], rhs=S.xn[:, kt, :],
                            start=(kt == 0), stop=(kt == KT - 1)), reads=[wa_b, S.xn_b[kt]], writes=[ppb])
                    c.op("dve", lambda e, dt=dt, pp=pp: e.tensor_tensor(
                        out=S.xT[:, dt, :], in0=S.xT[:, dt, :], in1=pp[:, :TT], op=ALU.add),
                        reads=[ppb, S.xT_b[dt]], writes=[S.xT_b[dt]])
            base += 4
            rmsnorm_fm(c, S, S.gains, 2)
            base = ffn_fm(c, S, ws, base)
            rmsnorm_fm(c, S, S.gains, 3, out32=True)
            for s in range(NSUB):
                for i in range(4):
                    for j in range(4):
                        dk = 4 * i + j
                        c.op("pe", lambda e, dk=dk, j=j, s=s: e.transpose(
                            out=ptr[:, j * 128:(j + 1) * 128], in_=S.xT[:, dk, s * 128:(s + 1) * 128],
                            identity=S.ident32[:, :]), reads=[S.xT_b[dk], S.const_b], writes=[ptr_b])
                    c.op("dve", lambda e, i=i: e.tensor_copy(out=S.xs[:, i * 512:(i + 1) * 512], in_=ptr[:, :]),
                         reads=[ptr_b], writes=[S.xs_b])
                c.dma("sp", Dr["out"][t0 + s * 128:t0 + (s + 1) * 128, :], S.xs[:, :], reads=[S.xs_b])
        c.barrier()

HD = 128; NH = 8; NG = 2; NCMP = 127; NBLK = 32
SCALE = HD ** -0.5
NEG = -30000.0
LSW = 1536
LC = 4096
YW = 1408


def _t5_bucket_np(dist):
    import math
    n = np.maximum(dist, 0)
    nf = np.maximum(n, 1).astype(np.float32)
    large = 16 + (np.log(nf / np.float32(16)) / np.float32(math.log(128 / 16)) * np.float32(16)).astype(np.int32)
    large = np.minimum(large, 31)
    return np.where(n < 16, n, large)


def nsa_consts():
    k = {}
    def onehot(d, masked):
        oh = np.zeros((33, d.shape[0]), np.float32)
        b = _t5_bucket_np(d)
        for i in range(d.shape[0]):
            if masked[i]:
                oh[32, i] = 1.0
            else:
                oh[b[i], i] = 1.0
        return oh
    d = np.arange(LSW) - 511
    k["oh_sel"] = onehot(d, d < 0)
    k["oh_win"] = onehot(d, (d < 0) | (d >= 512))
    d = np.arange(LC) - 2047
    k["oh_cmp"] = onehot(d, d < 0)
    k["J128"] = np.eye(128, dtype=np.float32)[::-1].copy()
    j127 = np.zeros((128, 128), np.float32); j127[:127, :127] = np.eye(127, dtype=np.float32)[::-1]
    k["J127"] = j127
    cs = np.arange(NCMP) * 16; js = np.arange(NBLK) * 64
    ov = np.clip(np.minimum(cs[:, None] + 32, js[None, :] + 64) - np.maximum(cs[:, None], js[None, :]), 0, None)
    ovl = np.zeros((128, 33), np.float32); ovl[:NCMP, :32] = ov.astype(np.float32) / 32.0; ovl[:NCMP, 32] = 1.0
    k["ovl"] = ovl
    t = np.arange(T); jj = np.arange(NBLK); cur = t // 64
    forced = (jj[None, :] == 0) | (jj[None, :] == cur[:, None]) | (jj[None, :] == cur[:, None] - 1)
    causal = js[None, :] <= t[:, None]
    cm = (causal & ~forced).astype(np.float32)
    ad = np.where(forced, 1e6, np.where(causal, 0.0, -1e9)).astype(np.float32)
    k["tk_mul"] = np.ascontiguousarray(cm.reshape(16, 128, 32).transpose(1, 0, 2))
    k["tk_add"] = np.ascontiguousarray(ad.reshape(16, 128, 32).transpose(1, 0, 2))
    ex = np.zeros((32, 16, 128), np.float32)
    for kt in range(16):
        for kk in range(128):
            ex[2 * kt + kk // 64, kt, kk] = -32768.0
    k["expneg"] = ex
    sg = np.zeros((24, 24, 128), np.float32)
    for r in range(24):
        sg[r, r, :] = 1.0
    k["selg"] = sg
    return k


def nsa_inputs(din):
    din("oh_sel", [33, LSW]); din("oh_win", [33, LSW]); din("oh_cmp", [33, LC])
    din("J128", [128, 128]); din("J127", [128, 128]); din("ovl", [128, 33])
    din("tk_mul", [128, 16, 32]); din("tk_add", [128, 16, 32]); din("expneg", [32, 16, 128]); din("selg", [24, 24, 128])
    din("rel_bias", [32, 8])
    for kv in ("k", "v"):
        din(f"cmp_w1_{kv}", [4096, 128]); din(f"cmp_w2_{kv}", [128, 128])
        din(f"cmp_peT_{kv}", [128, 32]); din(f"cmp_b1_{kv}", [128, 1])


def nsa_scratch(dscr):
    dscr("Fsel", [8, LSW], F32); dscr("Fwin", [8, LSW], F32); dscr("Fcmp", [8, LC], F32)


def nsa_host(m, inp):
    m.update(nsa_consts())
    m["rel_bias"] = np.ascontiguousarray(np.asarray(inp["rel_bias"], np.float32))
    for kv in ("k", "v"):
        m[f"cmp_w1_{kv}"] = np.ascontiguousarray(np.asarray(inp[f"cmp_w1_{kv}"])[0])
        m[f"cmp_w2_{kv}"] = np.ascontiguousarray(np.asarray(inp[f"cmp_w2_{kv}"])[0])
        m[f"cmp_peT_{kv}"] = np.ascontiguousarray(np.asarray(inp[f"cmp_pe_{kv}"])[0].T)
        m[f"cmp_b1_{kv}"] = np.ascontiguousarray(np.asarray(inp[f"cmp_b1_{kv}"])[0].reshape(128, 1))


def phase_NSA(c, nc, G, Dr):
    with ExitStack() as es:
        def tb(name, shape, dt):
            return sbt(nc, es, name, shape, dt), Buf(name)
        PS = {n: G.psum[i] for i, n in enumerate(("s0", "s1", "o0", "o1", "r0", "r1", "gb", "misc"))}
        cb = Buf("nsa_const")
        ones32, _ = tb("ones32", [128, 128], F32)
        c.op("dve", lambda e: e.memset(ones32[:, :], 1.0), writes=[cb])
        def load_const(name, shape, dt=F32, eng="sp", src=None):
            t_, _ = tb(name, shape, dt)
            c.dma(eng, t_[tuple(slice(None) for _ in shape)], src if src is not None else Dr[name][tuple(slice(None) for _ in shape)], writes=[cb])
            return t_
        J128 = load_const("J128", [128, 128]); J127 = load_const("J127", [128, 128])
        ovl = load_const("ovl", [128, 33])
        tkm = load_const("tk_mul", [128, 16, 32]); tka = load_const("tk_add", [128, 16, 32])
        expneg = load_const("expneg", [32, 16, 128], BF16, eng="pool")
        selg = load_const("selg", [24, 24, 128])
        cfar = load_const("cfar", [128, 8], src=Dr["rel_bias"][31:32, :].partition_broadcast(128))
        text, _ = tb("text", [33, 8], F32)
        c.op("dve", lambda e: e.memset(text[32:33, :], NEG), writes=[cb])
        c.dma("sp", text[0:32, :], Dr["rel_bias"][:, :], writes=[cb])
        w1 = {}; w2 = {}; peT = {}; b1 = {}
        for kv in ("k", "v"):
            w1[kv] = load_const(f"w1{kv}", [128, 32, 128], BF16, eng="pool",
                                src=Dr[f"cmp_w1_{kv}"].rearrange("(l d) j -> d l j", d=128))
            peT[kv] = load_const(f"peT{kv}", [128, 32], BF16, eng="pool", src=Dr[f"cmp_peT_{kv}"][:, :])
            b1[kv] = load_const(f"b1{kv}", [128, 1], src=Dr[f"cmp_b1_{kv}"][:, :])
        w2["k"] = load_const("w2k", [128, 128], BF16, eng="pool", src=Dr["cmp_w2_k"][:, :])
        w2["v"] = load_const("w2v", [128, 128], BF16, eng="pool", src=Dr["cmp_w2_v"][:, :])

        ohs = [tb(f"oh{i}", [33, 512], F32) for i in range(2)]
        fsb = [tb(f"fsb{i}", [8, 512], F32) for i in range(2)]
        pm, pmb = PS["misc"]
        nf = 0
        for nm, scr, L in (("oh_sel", "Fsel", LSW), ("oh_win", "Fwin", LSW), ("oh_cmp", "Fcmp", LC)):
            for ch in range(L // 512):
                oh, ohb = ohs[nf % 2]
                c.dma("sp", oh[:, :], Dr[nm][:, ch * 512:(ch + 1) * 512], writes=[ohb])
                c.op("pe", lambda e, oh=oh: e.matmul(pm[0:8, :], lhsT=text[:, :], rhs=oh[:, :],
                                                     start=True, stop=True), reads=[ohb, cb], writes=[pmb])
                ft_, ftb = fsb[nf % 2]; nf += 1
                c.op("dve", lambda e, ft_=ft_: e.tensor_copy(out=ft_[:, :], in_=pm[0:8, :]), reads=[pmb], writes=[ftb])
                c.dma("sp", Dr[scr][:, ch * 512:(ch + 1) * 512], ft_[:, :], reads=[ftb], writes=[cb])

        gsig, gsb = tb("gsig", [24, T], F32)
        c.dma("sp", gsig[:, :], Dr["gT"][0:24, :], writes=[gsb])
        c.op("act", lambda e: e.activation(out=gsig[:, :], in_=gsig[:, :], func=AF.Sigmoid), reads=[gsb], writes=[gsb])

        qT = [tb(f"qT{i}", [128, T], BF16) for i in range(4)]
        kT = {n: tb(f"kT{n}", [128, T], BF16) for n in ("c", "vc", "s", "w")}
        vtm = {n: tb(f"vtm{n}", [128, 16, 128], BF16) for n in ("s", "w")}
        kcbT, kcbb = tb("kcbT", [128, 128], BF16)
        vcb, vcbb = tb("vcb", [128, 128], F32)
        hid, hidb = tb("hid", [128, 128], BF16)
        bcol, bcolb = tb("bcol", [128, 1], F32)
        aacc, _ = tb("ccmp", [128, 4, T], BF16)
        aacc_b = [[Buf() for _ in range(4)] for _ in range(4)]
        acc32, acc32b = tb("acc32", [128, 512], F32)
        abf = [tb(f"abf{i}", [128, T], BF16) for i in range(1)]
        impa, impab = tb("impa", [128, 16, 32], F32)
        imp2, imp2b = tb("imp2", [128, 16, 32], F32)
        wk2, wk2b = tb("wk2", [128, 32], F32)
        m8, m8b = tb("m8", [128, 16], F32)
        nsel, nselb = tb("nsel", [128, 32], F32)
        nselT, nselTb = tb("nselT", [32, T], BF16)
        rec4, rec4b = tb("rec4", [128, 4], F32)
        hk, hkb = tb("hk", [128, 2048], F32)
        sstrip = [tb(f"sstrip{i}", [128, YW], F32) for i in range(2)]
        wstrip = [tb(f"wstrip{i}", [128, YW], F32) for i in range(2)]
        cstrip = [tb(f"cstrip{i}", [128, T], F32) for i in range(2)]
        tmpf = [tb(f"tmpf{i}", [128, 512], F32) for i in range(5)]
        pbf = [tb(f"pbf{i}", [128, 512], BF16) for i in range(8)]
        pf32 = [tb(f"pf32{i}", [128, 512], F32) for i in range(3)]
        rr, rrb = tb("rr", [128, 512], F32)
        wgt, wgtb = tb("wgt", [128, 512], F32)
        ctr = {"tmp": 0, "p": 0, "pf": 0, "s": 0, "or": 0}

        def rotp(lst, key):
            r = lst[ctr[key] % len(lst)]; ctr[key] += 1
            return r

        def combine(h, hl, qc, br, po, pob, pr, prb, first):
            pg, pgb = PS["gb"]
            r = 3 * h + br
            c.op("pe", lambda e: e.matmul(pg[:, :], lhsT=selg[:, r, :], rhs=gsig[:, qc * 512:(qc + 1) * 512],
                                          start=True, stop=True), reads=[cb, gsb], writes=[pgb])
            c.op("dve", lambda e: e.tensor_scalar_max(out=rr[:, :], in0=pr[:, :], scalar1=1e-30), reads=[prb], writes=[rrb])
            c.op("dve", lambda e: e.reciprocal(out=rr[:, :], in_=rr[:, :]), reads=[rrb], writes=[rrb])
            c.op("dve", lambda e: e.tensor_tensor(out=wgt[:, :], in0=rr[:, :], in1=pg[:, :], op=ALU.mult),
                 reads=[rrb, pgb], writes=[wgtb])
            dst = aacc[:, hl, qc * 512:(qc + 1) * 512]
            if br == 0:
                c.op("dve", lambda e: e.tensor_tensor(out=dst, in0=wgt[:, :], in1=po[:, :], op=ALU.mult),
                     reads=[wgtb, pob], writes=[aacc_b[hl][qc]])
            elif br == 2:
                c.op("dve", lambda e: e.tensor_tensor(out=acc32[:, :], in0=wgt[:, :], in1=po[:, :], op=ALU.mult),
                     reads=[wgtb, pob], writes=[acc32b])
            else:
                ab_, abb_ = abf[0]
                c.op("dve", lambda e: e.tensor_tensor(out=wgt[:, :], in0=wgt[:, :], in1=po[:, :], op=ALU.mult),
                     reads=[wgtb, pob], writes=[wgtb])
                c.op("pool", lambda e: e.tensor_tensor(out=acc32[:, :], in0=acc32[:, :], in1=wgt[:, :], op=ALU.add),
                     reads=[wgtb, acc32b], writes=[acc32b])
                c.op("dve", lambda e: e.tensor_tensor(out=ab_[:, qc * 512:(qc + 1) * 512], in0=acc32[:, :], in1=dst, op=ALU.add),
                     reads=[acc32b, aacc_b[hl][qc]], writes=[abb_])

        def build_strip(F_ap_tensor, h, L, cstep, nrows, width, Jt, dst, dstb):
            src = bass.AP(tensor=F_ap_tensor, offset=h * L, ap=[[cstep, nrows], [1, width]])
            c.dma("sp", hk[0:nrows, 0:width], src, reads=[cb], writes=[hkb])
            o = 0
            while o < width:
                n = min(512, width - o)
                ps_, psb_ = rotp([PS["s0"], PS["s1"]], "s")
                c.op("pe", lambda e, o=o, n=n, ps_=ps_: e.matmul(ps_[0:nrows, 0:n], lhsT=Jt[0:nrows, 0:nrows],
                                                                   rhs=hk[0:nrows, o:o + n], start=True, stop=True),
                     reads=[hkb, cb], writes=[psb_])
                c.op("act", lambda e, o=o, n=n, ps_=ps_: e.copy(out=dst[0:nrows, o:o + n], in_=ps_[0:nrows, 0:n]),
                     reads=[psb_], writes=[dstb])
                o += n

        nhead = 0
        for g in range(NG):
            for n, row in (("c", 1024), ("vc", 1280), ("s", 1536), ("w", 1792)):
                c.dma("sp", kT[n][0][:, :], Dr["projT"][row + g * 128:row + (g + 1) * 128, :], writes=[kT[n][1]])
            for n, col in (("s", 0), ("w", 256)):
                c.dma("sp", vtm[n][0][:, :, :],
                      Dr["vtm"].rearrange("(kt p) n -> p kt n", p=128)[:, :, col + g * 128:col + (g + 1) * 128],
                      writes=[vtm[n][1]])
            for hl in range(4):
                h = 4 * g + hl
                c.dma("sp", qT[hl][0][:, :], Dr["projT"][h * 128:(h + 1) * 128, :], writes=[qT[hl][1]])
            for kv, src in (("k", "c"), ("v", "vc")):
                kt_, ktb = kT[src]
                ph, phb = PS["misc"]
                for l in range(32):
                    c.op("pe", lambda e, l=l: e.matmul(ph[:, 0:NCMP], lhsT=w1[kv][:, l, :],
                                                      rhs=kt_[:, l:l + 16 * (NCMP - 1) + 1:16],
                                                      start=(l == 0), stop=(l == 31)), reads=[cb, ktb], writes=[phb])
                pb_, pbb_ = PS["gb"]
                for l in range(32):
                    c.op("pe", lambda e, l=l: e.matmul(pb_[:, 0:1], lhsT=w1[kv][:, l, :], rhs=peT[kv][:, l:l + 1],
                                                      start=(l == 0), stop=(l == 31)), reads=[cb], writes=[pbb_])
                c.op("dve", lambda e: e.tensor_tensor(out=bcol[:, :], in0=b1[kv][:, :], in1=pb_[:, 0:1], op=ALU.add),
                     reads=[cb, pbb_], writes=[bcolb])
                c.op("act", lambda e: e.activation(out=hid[:, 0:NCMP], in_=ph[:, 0:NCMP], func=AF.Gelu_apprx_tanh,
                                                   bias=bcol[:, 0:1]), reads=[phb, bcolb], writes=[hidb])
                po_, pob_ = PS["o0"]
                if kv == "k":
                    c.op("pe", lambda e: e.matmul(po_[:, 0:NCMP], lhsT=w2["k"][:, :], rhs=hid[:, 0:NCMP],
                                                  start=True, stop=True), reads=[cb, hidb], writes=[pob_])
                    c.op("dve", lambda e: e.tensor_copy(out=kcbT[:, 0:NCMP], in_=po_[:, 0:NCMP]),
                         reads=[pob_], writes=[kcbb])
                else:
                    c.op("pe", lambda e: e.matmul(po_[0:NCMP, 0:128], lhsT=hid[:, 0:NCMP], rhs=w2["v"][:, :],
                                                  start=True, stop=True), reads=[cb, hidb], writes=[pob_])
                    c.op("dve", lambda e: e.tensor_copy(out=vcb[0:NCMP, :], in_=po_[0:NCMP, 0:128]),
                         reads=[pob_], writes=[vcbb])
            c.op("pool", lambda e: e.memset(impa[:, :, :], 0.0), writes=[impab])
            p1pend = []

            def p1_stage1(h, hl, qc, cs_, csb_):
                q_, qb_ = qT[hl]
                ps_, psb_ = rotp([PS["s0"], PS["s1"]], "s")
                c.op("pe", lambda e: e.matmul(ps_[0:NCMP, :], lhsT=kcbT[:, 0:NCMP], rhs=q_[:, qc * 512:(qc + 1) * 512],
                                              start=True, stop=True), reads=[kcbb, qb_], writes=[psb_])
                tf, tfb = rotp(tmpf, "tmp")
                c.op("dve", lambda e: e.scalar_tensor_tensor(out=tf[0:NCMP, :], in0=ps_[0:NCMP, :], scalar=SCALE,
                                                             in1=cs_[0:NCMP, qc * 512:(qc + 1) * 512],
                                                             op0=ALU.mult, op1=ALU.add),
                     reads=[psb_, csb_], writes=[tfb])
                pf, pfb = rotp(pf32, "pf")
                c.op("act", lambda e: e.activation(out=pf[0:NCMP, :], in_=tf[0:NCMP, :], func=AF.Exp),
                     reads=[tfb], writes=[pfb])
                return (h, hl, qc, pf, pfb)

            def p1_stage2(h, hl, qc, pf, pfb):
                po, pob = PS["o0"] if ctr["or"] % 2 == 0 else PS["o1"]
                pr, prb = PS["r0"] if ctr["or"] % 2 == 0 else PS["r1"]
                ctr["or"] += 1
                c.op("pe", lambda e: e.matmul(po[:, :], lhsT=vcb[0:NCMP, :], rhs=pf[0:NCMP, :], start=True, stop=True),
                     reads=[vcbb, pfb], writes=[pob])
                c.op("pe", lambda e: e.matmul(pr[:, :], lhsT=ones32[0:NCMP, :], rhs=pf[0:NCMP, :], start=True, stop=True),
                     reads=[cb, pfb], writes=[prb])
                pi, pib = PS["misc"]
                for s in range(4):
                    c.op("pe", lambda e, s=s: e.matmul(pi[:, s * 33:(s + 1) * 33], lhsT=pf[0:NCMP, s * 128:(s + 1) * 128],
                                                      rhs=ovl[0:NCMP, :], start=True, stop=True),
                         reads=[cb, pfb], writes=[pib])
                combine(h, hl, qc, 0, po, pob, pr, prb, True)
                piv = pi[:, 0:132].rearrange("p (s j) -> p s j", j=33)
                c.op("dve", lambda e: e.tensor_scalar_max(out=rec4[:, :].rearrange("p (s o) -> p s o", o=1),
                                                          in0=piv[:, :, 32:33], scalar1=1e-30),
                     reads=[pib], writes=[rec4b])
                c.op("dve", lambda e: e.reciprocal(out=rec4[:, :], in_=rec4[:, :]), reads=[rec4b], writes=[rec4b])
                for s in range(4):
                    c.op("dve", lambda e, s=s: e.scalar_tensor_tensor(
                        out=impa[:, 4 * qc + s, :], in0=pi[:, s * 33:s * 33 + 32], scalar=rec4[:, s:s + 1],
                        in1=impa[:, 4 * qc + s, :], op0=ALU.mult, op1=ALU.add),
                        reads=[pib, rec4b, impab], writes=[impab])

            for hl in range(4):
                h = 4 * g + hl
                cs_, csb_ = cstrip[hl % 2]
                build_strip(Dr["Fcmp"].tensor, h, LC, 16, NCMP, T, J127, cs_, csb_)
                for qc in range(4):
                    p1pend.append(p1_stage1(h, hl, qc, cs_, csb_))
                    while len(p1pend) > 1:
                        p1_stage2(*p1pend.pop(0))
                nhead += 1
            while p1pend:
                p1_stage2(*p1pend.pop(0))
            c.op("dve", lambda e: e.tensor_tensor(out=imp2[:, :, :], in0=impa[:, :, :], in1=tkm[:, :, :], op=ALU.mult),
                 reads=[impab, cb], writes=[imp2b])
            c.op("dve", lambda e: e.tensor_tensor(out=imp2[:, :, :], in0=imp2[:, :, :], in1=tka[:, :, :], op=ALU.add),
                 reads=[imp2b, cb], writes=[imp2b])
            for tt in range(16):
                c.op("dve", lambda e: e.max(out=m8[:, 0:8], in_=imp2[:, tt, :]), reads=[imp2b], writes=[m8b])
                c.op("dve", lambda e: e.match_replace(out=wk2[:, :], in_to_replace=m8[:, 0:8], in_values=imp2[:, tt, :],
                                                      imm_value=-3e38), reads=[imp2b, m8b], writes=[wk2b])
                c.op("dve", lambda e: e.max(out=m8[:, 8:16], in_=wk2[:, :]), reads=[wk2b], writes=[m8b])
                c.op("dve", lambda e: e.tensor_scalar(out=nsel[:, :], in0=imp2[:, tt, :], scalar1=m8[:, 15:16], scalar2=None,
                                                      op0=ALU.is_lt), reads=[imp2b, m8b], writes=[nselb])
                pt, ptb = PS["misc"]
                c.op("pe", lambda e: e.transpose(out=pt[0:32, 0:128], in_=nsel[:, :], identity=G.ident32[:, :]),
                     reads=[nselb, G.const_b], writes=[ptb])
                c.op("act", lambda e: e.copy(out=nselT[:, tt * 128:(tt + 1) * 128], in_=pt[0:32, 0:128]),
                     reads=[ptb], writes=[nselTb])
            LOOK = 3
            sbanks = [PS["s0"], PS["s1"], PS["misc"]]
            pend = []
            grp = [0]

            def stage1(u):
                h, hl, qc, br, kt, ii, n, gi, strip_, stripb_ = u
                q_, qb_ = qT[hl]
                kname = "w" if br == 2 else "s"
                k_, kb_ = kT[kname]
                ps_, psb_ = rotp(sbanks, "s")
                c.op("pe", lambda e: e.matmul(ps_[:, :], lhsT=k_[:, kt * 128:(kt + 1) * 128],
                                              rhs=q_[:, qc * 512:(qc + 1) * 512], start=True, stop=(br == 2)),
                     reads=[kb_, qb_], writes=[psb_])
                if br == 1:
                    c.op("pe", lambda e: e.matmul(ps_[:, :], lhsT=expneg[:, kt, :], rhs=nselT[:, qc * 512:(qc + 1) * 512],
                                                  start=False, stop=True), reads=[cb, nselTb], writes=[psb_])
                pb_, pbb_ = rotp(pbf, "p")
                if br == 2 or kt >= 4 * qc - 1:
                    yoff = 512 * qc - 128 * kt + 384
                    tf, tfb = rotp(tmpf, "tmp")
                    c.op("dve", lambda e: e.scalar_tensor_tensor(out=tf[:, :], in0=ps_[:, :], scalar=SCALE,
                                                                 in1=strip_[:, yoff:yoff + 512], op0=ALU.mult, op1=ALU.add),
                         reads=[psb_, stripb_], writes=[tfb])
                    c.op("act", lambda e: e.activation(out=pb_[:, :], in_=tf[:, :], func=AF.Exp), reads=[tfb], writes=[pbb_])
                else:
                    c.op("act", lambda e: e.activation(out=pb_[:, :], in_=ps_[:, :], func=AF.Exp, bias=cfar[:, h:h + 1],
                                                       scale=SCALE), reads=[psb_, cb], writes=[pbb_])
                return (pb_, pbb_)

            def stage2(u, pp):
                h, hl, qc, br, kt, ii, n, gi, strip_, stripb_ = u
                pb_, pbb_ = pp
                v_, vb_ = vtm["w" if br == 2 else "s"]
                po, pob = PS["o0"] if gi % 2 == 0 else PS["o1"]
                pr, prb = PS["r0"] if gi % 2 == 0 else PS["r1"]
                c.op("pe", lambda e: e.matmul(po[:, :], lhsT=v_[:, kt, :], rhs=pb_[:, :], start=(ii == 0), stop=(ii == n - 1)),
                     reads=[vb_, pbb_], writes=[pob])
                c.op("pe", lambda e: e.matmul(pr[:, :], lhsT=G.ones_bf[:, :], rhs=pb_[:, :], start=(ii == 0), stop=(ii == n - 1)),
                     reads=[G.const_b, pbb_], writes=[prb])
                if ii == n - 1:
                    combine(h, hl, qc, br, po, pob, pr, prb, False)

            def flush(keep):
                while len(pend) > keep:
                    u, pp = pend.pop(0)
                    stage2(u, pp)

            for hl in range(4):
                h = 4 * g + hl
                ss_, ssb_ = sstrip[h % 2]; wsr_, wsb_ = wstrip[h % 2]
                build_strip(Dr["Fsel"].tensor, h, LSW, 1, 128, YW, J128, ss_, ssb_)
                build_strip(Dr["Fwin"].tensor, h, LSW, 1, 128, YW, J128, wsr_, wsb_)
                for qc in range(4):
                    for br, strip_, stripb_ in ((2, wsr_, wsb_), (1, ss_, ssb_)):
                        kts = list(range(max(0, 4 * qc - 4), 4 * qc + 4)) if br == 2 else list(range(0, 4 * qc + 4))
                        gi = grp[0]; grp[0] += 1
                        for ii, kt in enumerate(kts):
                            u = (h, hl, qc, br, kt, ii, len(kts), gi, strip_, stripb_)
                            pend.append((u, stage1(u)))
                            flush(LOOK)
                flush(0)
                ab_, abb_ = abf[0]
                c.dma("sp", Dr["mixT"][h * 128:(h + 1) * 128, :], ab_[:, :], reads=[abb_])
        c.barrier()

import math
TWO_PI = 2.0 * math.pi
NPAIR = 32
STAG = 14


def s5_inputs(din):
    for n in ("lre_b", "lim_b", "lstep_b", "bT_re", "bT_im"):
        din(n, [128, 8, 64])
    for n in ("lreP", "limP", "lstepP"):
        din(n, [128, NPAIR])
    din("cpad_re", [128, NPAIR, 128]); din("cpad_im", [128, NPAIR, 128])
    din("maskcol", [128, 8]); din("dcol", [128, 8]); din("glub_col", [128, 8]); din("iota512", [128, 512])
    din("glu_w", [1024, 1024])


def s5_host(m, inp):
    f = lambda n: np.asarray(inp[n], np.float32)[0]
    lre, lim, lst = f("ssm_lam_re"), f("ssm_lam_im"), f("ssm_log_step")
    bre, bim, cre, cim = f("ssm_b_re"), f("ssm_b_im"), f("ssm_c_re"), f("ssm_c_im")
    def lay_b(a):
        out = np.zeros((128, 8, 64), np.float32)
        for g in range(64):
            out[(g % 8) * 16:(g % 8) * 16 + 16, g // 8, :] = a[g][None, :]
        return out
    m["lre_b"] = lay_b(lre); m["lim_b"] = lay_b(lim); m["lstep_b"] = lay_b(np.repeat(lst[:, None], 64, axis=1))
    def lay_bT(b):
        out = np.zeros((128, 8, 64), np.float32)
        for g in range(64):
            out[(g % 8) * 16:(g % 8) * 16 + 16, g // 8, :] = b[g].T
        return out
    m["bT_re"] = lay_bT(bre); m["bT_im"] = lay_bT(bim)
    def lay_P(a):
        out = np.zeros((128, NPAIR), np.float32)
        for g in range(64):
            out[(g % 2) * 64:(g % 2) * 64 + 64, g // 2] = a[g]
        return out
    m["lreP"] = lay_P(lre); m["limP"] = lay_P(lim); m["lstepP"] = lay_P(np.repeat(lst[:, None], 64, axis=1))
    def lay_c(cc):
        out = np.zeros((128, NPAIR, 128), np.float32)
        for g in range(64):
            half = g % 2; g8 = g % 8
            out[half * 64:half * 64 + 64, g // 2, 16 * g8:16 * g8 + 16] = cc[g].T
        return out
    m["cpad_re"] = lay_c(cre); m["cpad_im"] = lay_c(cim)
    mc = np.zeros((128, 8), np.float32)
    for g8 in range(8):
        mc[g8 * 16:(g8 + 1) * 16, g8] = 1.0
    m["maskcol"] = mc
    m["dcol"] = np.ascontiguousarray(f("ssm_d").reshape(8, 128).T)
    m["glub_col"] = np.ascontiguousarray(f("glu_b").reshape(8, 128).T)
    m["iota512"] = np.ascontiguousarray(np.tile(np.arange(512, dtype=np.float32)[None, :], (128, 1)))
    m["glu_w"] = f("glu_w")


def phase_S5(c, nc, G, Dr):
    with ExitStack() as es:
        def tb(name, shape, dt):
            return sbt(nc, es, name, shape, dt), Buf(name)
        cb = Buf("s5const")
        def ld(name, shape, dt=F32, eng="sp", src=None):
            t_, _ = tb(name, shape, dt)
            idx = tuple(slice(None) for _ in shape)
            c.dma(eng, t_[idx], src if src is not None else Dr[name][idx], writes=[cb])
            return t_
        lre = ld("lre_b", [128, 512], src=Dr["lre_b"].rearrange("p a b -> p (a b)"))
        lim = ld("lim_b", [128, 512], src=Dr["lim_b"].rearrange("p a b -> p (a b)"))
        lst = ld("lstep_b", [128, 512], src=Dr["lstep_b"].rearrange("p a b -> p (a b)"))
        bre = ld("bT_re", [128, 512], src=Dr["bT_re"].rearrange("p a b -> p (a b)"))
        bim = ld("bT_im", [128, 512], src=Dr["bT_im"].rearrange("p a b -> p (a b)"))
        lreP = ld("lreP", [128, NPAIR]); limP = ld("limP", [128, NPAIR]); lstP = ld("lstepP", [128, NPAIR])
        cre = ld("cpad_re", [128, NPAIR, 128], BF16, eng="pool")
        cim = ld("cpad_im", [128, NPAIR, 128], BF16, eng="pool")
        maskcol = ld("maskcol", [128, 8]); dcol = ld("dcol", [128, 8]); gbcol = ld("glub_col", [128, 8])
        iota = ld("iota512", [128, 512])
        gw = ld("glu_w", [128, 8, 1024], BF16, eng="pool", src=Dr["glu_w"].rearrange("(kt p) n -> p kt n", p=128))
        c.op("dve", lambda e: e.tensor_scalar(out=cim[:, :, :], in0=cim[:, :, :], scalar1=-1.0, scalar2=None, op0=ALU.mult),
             reads=[cb], writes=[cb])
        halfpi, _ = tb("halfpi", [128, 1], F32)
        c.op("dve", lambda e: e.memset(halfpi[:, :], math.pi / 2), writes=[cb])

        W = {}
        def wt(n, shape=(128, 512), dt=F32):
            W[n] = tb("w_" + n, list(shape), dt)
            return W[n][0]
        wb = Buf("s5work")

        def dv(fn):
            c.op("dve", fn, reads=[cb, wb], writes=[wb])

        def ac(fn):
            c.op("act", fn, reads=[cb, wb], writes=[wb])

        def sincos(theta, sin_o, cos_o, q, qi, fr, n):
            dv(lambda e: e.tensor_scalar(out=q[:, :n], in0=theta[:, :n], scalar1=1.0 / TWO_PI, scalar2=None, op0=ALU.mult))
            dv(lambda e: e.tensor_copy(out=qi[:, :n], in_=q[:, :n]))
            dv(lambda e: e.tensor_copy(out=fr[:, :n], in_=qi[:, :n]))
            dv(lambda e: e.tensor_tensor(out=fr[:, :n], in0=q[:, :n], in1=fr[:, :n], op=ALU.subtract))
            ac(lambda e: e.activation(out=sin_o[:, :n], in_=fr[:, :n], func=AF.Sin, scale=TWO_PI))
            dv(lambda e: e.tensor_scalar(out=q[:, :n], in0=fr[:, :n], scalar1=-1.0, scalar2=None, op0=ALU.mult))
            dv(lambda e: e.tensor_tensor(out=fr[:, :n], in0=fr[:, :n], in1=q[:, :n], op=ALU.max))
            ac(lambda e: e.activation(out=cos_o[:, :n], in_=fr[:, :n], func=AF.Sin, scale=-TWO_PI, bias=halfpi[:, 0:1]))

        step = wt("step"); rr_ = wt("r"); th = wt("th"); sn = wt("sn"); cs = wt("cs"); q_ = wt("q"); fr = wt("fr")
        qi = wt("qi", dt=I32); t1 = wt("t1"); t2 = wt("t2"); fre = wt("fre"); fim = wt("fim")
        BbR = wt("BbR"); BbI = wt("BbI")
        ac(lambda e: e.activation(out=step[:, :], in_=lst[:, :], func=AF.Exp))
        dv(lambda e: e.tensor_tensor(out=t1[:, :], in0=lre[:, :], in1=step[:, :], op=ALU.mult))
        ac(lambda e: e.activation(out=rr_[:, :], in_=t1[:, :], func=AF.Exp))
        dv(lambda e: e.tensor_tensor(out=th[:, :], in0=lim[:, :], in1=step[:, :], op=ALU.mult))
        sincos(th, sn, cs, q_, qi, fr, 512)
        dv(lambda e: e.tensor_tensor(out=cs[:, :], in0=cs[:, :], in1=rr_[:, :], op=ALU.mult))
        dv(lambda e: e.tensor_scalar(out=cs[:, :], in0=cs[:, :], scalar1=-1.0, scalar2=None, op0=ALU.add))
        dv(lambda e: e.tensor_tensor(out=sn[:, :], in0=sn[:, :], in1=rr_[:, :], op=ALU.mult))
        dv(lambda e: e.tensor_tensor(out=t1[:, :], in0=lre[:, :], in1=lre[:, :], op=ALU.mult))
        dv(lambda e: e.tensor_tensor(out=t2[:, :], in0=lim[:, :], in1=lim[:, :], op=ALU.mult))
        dv(lambda e: e.tensor_tensor(out=t1[:, :], in0=t1[:, :], in1=t2[:, :], op=ALU.add))
        dv(lambda e: e.reciprocal(out=t1[:, :], in_=t1[:, :]))
        dv(lambda e: e.tensor_tensor(out=fre[:, :], in0=cs[:, :], in1=lre[:, :], op=ALU.mult))
        dv(lambda e: e.tensor_tensor(out=t2[:, :], in0=sn[:, :], in1=lim[:, :], op=ALU.mult))
        dv(lambda e: e.tensor_tensor(out=fre[:, :], in0=fre[:, :], in1=t2[:, :], op=ALU.add))
        dv(lambda e: e.tensor_tensor(out=fre[:, :], in0=fre[:, :], in1=t1[:, :], op=ALU.mult))
        dv(lambda e: e.tensor_tensor(out=fim[:, :], in0=sn[:, :], in1=lre[:, :], op=ALU.mult))
        dv(lambda e: e.tensor_tensor(out=t2[:, :], in0=cs[:, :], in1=lim[:, :], op=ALU.mult))
        dv(lambda e: e.tensor_tensor(out=fim[:, :], in0=fim[:, :], in1=t2[:, :], op=ALU.subtract))
        dv(lambda e: e.tensor_tensor(out=fim[:, :], in0=fim[:, :], in1=t1[:, :], op=ALU.mult))
        dv(lambda e: e.tensor_tensor(out=BbR[:, :], in0=fre[:, :], in1=bre[:, :], op=ALU.mult))
        dv(lambda e: e.tensor_tensor(out=t2[:, :], in0=fim[:, :], in1=bim[:, :], op=ALU.mult))
        dv(lambda e: e.tensor_tensor(out=BbR[:, :], in0=BbR[:, :], in1=t2[:, :], op=ALU.subtract))
        dv(lambda e: e.tensor_tensor(out=BbI[:, :], in0=fre[:, :], in1=bim[:, :], op=ALU.mult))
        dv(lambda e: e.tensor_tensor(out=t2[:, :], in0=fim[:, :], in1=bre[:, :], op=ALU.mult))
        dv(lambda e: e.tensor_tensor(out=BbI[:, :], in0=BbI[:, :], in1=t2[:, :], op=ALU.add))
        lhsB, _ = tb("lhsB", [128, NPAIR, 2, 128], BF16)
        for pr in range(NPAIR):
            gt = pr // 4
            for half in range(2):
                g8 = 2 * (pr % 4) + half
                for ri, src in enumerate((BbR, BbI)):
                    dv(lambda e, pr=pr, half=half, g8=g8, ri=ri, src=src, gt=gt: e.tensor_scalar(
                        out=lhsB[:, pr, ri, half * 64:(half + 1) * 64], in0=src[:, gt * 64:(gt + 1) * 64],
                        scalar1=maskcol[:, g8:g8 + 1], scalar2=None, op0=ALU.mult))
        stP = wt("stP", (128, NPAIR)); rP = wt("rP", (128, NPAIR)); thq = wt("thq", (128, NPAIR))
        psi = wt("psi", (128, 4, NPAIR)); pq = wt("pq", (128, NPAIR)); pqi = wt("pqi", (128, NPAIR), I32)
        ac(lambda e: e.activation(out=stP[:, :], in_=lstP[:, :], func=AF.Exp))
        dv(lambda e: e.tensor_tensor(out=rP[:, :], in0=lreP[:, :], in1=stP[:, :], op=ALU.mult))
        ac(lambda e: e.activation(out=rP[:, :], in_=rP[:, :], func=AF.Exp))
        dv(lambda e: e.tensor_tensor(out=thq[:, :], in0=limP[:, :], in1=stP[:, :], op=ALU.mult))
        dv(lambda e: e.tensor_scalar(out=thq[:, :], in0=thq[:, :], scalar1=1.0 / TWO_PI, scalar2=None, op0=ALU.mult))
        dv(lambda e: e.memset(psi[:, 0, :], 0.0))
        for tc in range(1, 4):
            dv(lambda e, tc=tc: e.tensor_scalar(out=pq[:, :], in0=thq[:, :], scalar1=512.0 * tc, scalar2=None, op0=ALU.mult))
            dv(lambda e: e.tensor_copy(out=pqi[:, :], in_=pq[:, :]))
            dv(lambda e, tc=tc: e.tensor_copy(out=psi[:, tc, :], in_=pqi[:, :]))
            dv(lambda e, tc=tc: e.tensor_tensor(out=psi[:, tc, :], in0=pq[:, :], in1=psi[:, tc, :], op=ALU.subtract))

        uT = [tb(f"uT{i}", [128, T], BF16) for i in range(2)]
        hT, _ = tb("hT", [128, 8, T], BF16)
        hT_b = [[Buf() for _ in range(4)] for _ in range(8)]
        NB = 2
        Q = [tb(f"Q{i}", [128, 512], F32) for i in range(NB)]
        QI = [tb(f"QI{i}", [128, 512], I32) for i in range(NB)]
        FR = [tb(f"FR{i}", [128, 512], F32) for i in range(NB)]
        AB = [tb(f"AB{i}", [128, 512], F32) for i in range(NB)]
        SN = [tb(f"SN{i}", [128, 512], F32) for i in range(NB)]
        CS = [tb(f"CS{i}", [128, 512], F32) for i in range(NB)]
        TA = [tb(f"TA{i}", [128, 512], F32) for i in range(NB)]
        TB_ = [tb(f"TB{i}", [128, 512], F32) for i in range(NB)]
        TC = [tb(f"TC{i}", [128, 512], F32) for i in range(NB)]
        TD = [tb(f"TD{i}", [128, 512], F32) for i in range(NB)]
        XR = [tb(f"XR{i}", [128, 512], BF16) for i in range(NB)]
        XI = [tb(f"XI{i}", [128, 512], BF16) for i in range(NB)]
        ytmp = [tb(f"ytmp{i}", [128, 512], F32) for i in range(2)]
        PY = [G.psum[i] for i in range(4)]
        PB = [(G.psum[4], G.psum[5]), (G.psum[6], G.psum[7])]
        ZRc = [[tb(f"ZRc{i}_{j}", [128, 512], F32) for j in range(2)] for i in range(2)]
        ZIc = [[tb(f"ZIc{i}_{j}", [128, 512], F32) for j in range(2)] for i in range(2)]

        def run_rr(gens):
            gens = list(gens)
            while gens:
                for g_ in list(gens):
                    try:
                        next(g_)
                    except StopIteration:
                        gens.remove(g_)

        def chain(gt, ci, u_, ub_):
            i = ci
            for pl in (2 * ci, 2 * ci + 1):
                pr = gt * 4 + pl
                for tc in range(4):
                    (pbr, pbrb), (pbi, pbib) = PB[ci]
                    sl = slice(tc * 512, (tc + 1) * 512)
                    c.op("pe", lambda e: e.matmul(pbr[:, :], lhsT=lhsB[:, pr, 0, :], rhs=u_[:, sl], start=True, stop=True),
                         reads=[wb, ub_], writes=[pbrb])
                    yield
                    c.op("pe", lambda e: e.matmul(pbi[:, :], lhsT=lhsB[:, pr, 1, :], rhs=u_[:, sl], start=True, stop=True),
                         reads=[wb, ub_], writes=[pbib])
                    yield
                    q, qb = Q[i]; qi_, qib = QI[i]; fr_, frb = FR[i]; ab, abb = AB[i]; s_, sb_ = SN[i]; c_, cb_ = CS[i]
                    ta, tab = TA[i]; tb2, tbb = TB_[i]; tcc, tcb = TC[i]; td, tdb = TD[i]
                    zr, zrb = ZRc[ci][tc % 2]; zi, zib = ZIc[ci][tc % 2]; xr, xrb = XR[i]; xi, xib = XI[i]
                    c.op("act", lambda e: e.activation(out=q[:, :], in_=iota[:, :], func=AF.Identity, scale=thq[:, pr:pr + 1],
                                                       bias=psi[:, tc, pr:pr + 1]), reads=[cb, wb], writes=[qb])
                    yield
                    c.op("dve", lambda e: e.tensor_copy(out=qi_[:, :], in_=q[:, :]), reads=[qb], writes=[qib])
                    yield
                    c.op("dve", lambda e: e.tensor_copy(out=fr_[:, :], in_=qi_[:, :]), reads=[qib], writes=[frb])
                    yield
                    c.op("dve", lambda e: e.tensor_tensor(out=fr_[:, :], in0=q[:, :], in1=fr_[:, :], op=ALU.subtract),
                         reads=[qb, frb], writes=[frb])
                    yield
                    c.op("act", lambda e: e.activation(out=s_[:, :], in_=fr_[:, :], func=AF.Sin, scale=TWO_PI),
                         reads=[frb], writes=[sb_])
                    yield
                    c.op("act", lambda e: e.activation(out=ab[:, :], in_=fr_[:, :], func=AF.Sin, scale=math.pi),
                         reads=[frb], writes=[abb])
                    yield
                    c.op("act", lambda e: e.activation(out=ab[:, :], in_=ab[:, :], func=AF.Square),
                         reads=[abb], writes=[abb])
                    yield
                    c.op("act", lambda e: e.activation(out=c_[:, :], in_=ab[:, :], func=AF.Copy, scale=-2.0, bias=1.0),
                         reads=[abb], writes=[cb_])
                    yield
                    c.op("dve", lambda e: e.tensor_tensor(out=ta[:, :], in0=c_[:, :], in1=pbr[:, :], op=ALU.mult),
                         reads=[cb_, pbrb], writes=[tab])
                    yield
                    c.op("dve", lambda e: e.tensor_tensor(out=tb2[:, :], in0=s_[:, :], in1=pbi[:, :], op=ALU.mult),
                         reads=[sb_, pbib], writes=[tbb])
                    yield
                    c.op("dve", lambda e: e.tensor_tensor(out=tcc[:, :], in0=c_[:, :], in1=pbi[:, :], op=ALU.mult),
                         reads=[cb_, pbib], writes=[tcb])
                    yield
                    c.op("dve", lambda e: e.tensor_tensor(out=td[:, :], in0=s_[:, :], in1=pbr[:, :], op=ALU.mult),
                         reads=[sb_, pbrb], writes=[tdb])
                    yield
                    c.op("pool", lambda e: e.tensor_tensor(out=ta[:, :], in0=ta[:, :], in1=tb2[:, :], op=ALU.add),
                         reads=[tab, tbb], writes=[tab])
                    yield
                    c.op("pool", lambda e: e.tensor_tensor(out=tcc[:, :], in0=tcc[:, :], in1=td[:, :], op=ALU.subtract),
                         reads=[tcb, tdb], writes=[tcb])
                    yield
                    zrp, zrpb = ZRc[ci][(tc + 1) % 2]; zip_, zipb = ZIc[ci][(tc + 1) % 2]
                    init_r = 0.0 if tc == 0 else zrp[:, 511:512]
                    init_i = 0.0 if tc == 0 else zip_[:, 511:512]
                    rbc = rP[:, pr:pr + 1].to_broadcast([128, 512])
                    c.op("dve", lambda e: e.tensor_tensor_scan(out=zr[:, :], data0=rbc, data1=ta[:, :], initial=init_r,
                                                               op0=ALU.mult, op1=ALU.add),
                         reads=[tab, wb] + ([zrpb] if tc else []), writes=[zrb])
                    yield
                    c.op("dve", lambda e: e.tensor_tensor_scan(out=zi[:, :], data0=rbc, data1=tcc[:, :], initial=init_i,
                                                               op0=ALU.mult, op1=ALU.add),
                         reads=[tcb, wb] + ([zipb] if tc else []), writes=[zib])
                    yield
                    c.op("dve", lambda e: e.tensor_tensor(out=ta[:, :], in0=c_[:, :], in1=zr[:, :], op=ALU.mult),
                         reads=[cb_, zrb], writes=[tab])
                    yield
                    c.op("dve", lambda e: e.tensor_tensor(out=tb2[:, :], in0=s_[:, :], in1=zi[:, :], op=ALU.mult),
                         reads=[sb_, zib], writes=[tbb])
                    yield
                    c.op("pool", lambda e: e.tensor_tensor(out=xr[:, :], in0=ta[:, :], in1=tb2[:, :], op=ALU.subtract),
                         reads=[tab, tbb], writes=[xrb])
                    yield
                    c.op("dve", lambda e: e.tensor_tensor(out=tcc[:, :], in0=s_[:, :], in1=zr[:, :], op=ALU.mult),
                         reads=[sb_, zrb], writes=[tcb])
                    yield
                    c.op("dve", lambda e: e.tensor_tensor(out=td[:, :], in0=c_[:, :], in1=zi[:, :], op=ALU.mult),
                         reads=[cb_, zib], writes=[tdb])
                    yield
                    c.op("pool", lambda e: e.tensor_tensor(out=xi[:, :], in0=tcc[:, :], in1=td[:, :], op=ALU.add),
                         reads=[tcb, tdb], writes=[xib])
                    yield
                    py, pyb = PY[tc]
                    c.op("pe", lambda e: e.matmul(py[:, :], lhsT=cre[:, pr, :], rhs=xr[:, :], start=(pl == 0), stop=False),
                         reads=[cb, xrb], writes=[pyb])
                    yield
                    c.op("pe", lambda e: e.matmul(py[:, :], lhsT=cim[:, pr, :], rhs=xi[:, :], start=False, stop=(pl == 3)),
                         reads=[cb, xib], writes=[pyb])
                    yield

        for gt in range(8):
            u_, ub_ = uT[gt % 2]
            c.dma("sp", u_[:, :], Dr["projT"][2048 + gt * 128:2048 + (gt + 1) * 128, :], writes=[ub_])
            ga, gb_ = chain(gt, 0, u_, ub_), chain(gt, 1, u_, ub_)
            for _ in range(STAG):
                next(ga)
            run_rr([ga, gb_])
            for tc in range(4):
                py, pyb = PY[tc]
                yt, ytb = ytmp[tc % 2]
                sl = slice(tc * 512, (tc + 1) * 512)
                c.op("dve", lambda e: e.scalar_tensor_tensor(out=yt[:, :], in0=u_[:, sl], scalar=dcol[:, gt:gt + 1],
                                                             in1=py[:, :], op0=ALU.mult, op1=ALU.add),
                     reads=[ub_, cb, pyb], writes=[ytb])
                c.op("act", lambda e: e.activation(out=hT[:, gt, sl], in_=yt[:, :], func=AF.Gelu_apprx_tanh),
                     reads=[ytb], writes=[hT_b[gt][tc]])
        sg = [tb(f"sg{i}", [128, 512], F32) for i in range(2)]
        so = [tb(f"so{i}", [128, 512], BF16) for i in range(2)]
        k = 0
        for nt in range(8):
            for tc in range(4):
                pz, pzb = G.psum[k % 8]
                sl = slice(tc * 512, (tc + 1) * 512)
                for kt in range(8):
                    c.op("pe", lambda e, kt=kt: e.matmul(pz[:, :], lhsT=gw[:, kt, nt * 128:(nt + 1) * 128], rhs=hT[:, kt, sl],
                                                        start=(kt == 0), stop=(kt == 7)),
                         reads=[cb, hT_b[kt][tc]], writes=[pzb])
                sg_, sgb = sg[k % 2]; so_, sob = so[k % 2]
                c.op("act", lambda e: e.activation(out=sg_[:, :], in_=pz[:, :], func=AF.Sigmoid, bias=gbcol[:, nt:nt + 1]),
                     reads=[pzb, cb], writes=[sgb])
                c.op("dve", lambda e: e.tensor_tensor(out=so_[:, :], in0=sg_[:, :], in1=hT[:, nt, sl], op=ALU.mult),
                     reads=[sgb, hT_b[nt][tc]], writes=[sob])
                c.dma("sp", Dr["mixT"][1024 + nt * 128:1024 + (nt + 1) * 128, sl], so_[:, :], reads=[sob])
                k += 1
        c.barrier()


def extra_inputs(din):
    nsa_inputs(din); s5_inputs(din)
def extra_scratch(dscr):
    nsa_scratch(dscr)
def extra_host(m, inp):
    nsa_host(m, inp); s5_host(m, inp)

DEBUG_OUT = []
PHASES = ("A", "NSA", "S5", "D")
LAST_RES = None


def build_nc(needed=None, dry=False):
    nc = bass.Bass("TRN2", target_bir_lowering=False)
    Dr = {}

    def din(name, shape, dt=F32):
        Dr[name] = nc.dram_tensor(name, list(shape), dt, kind="ExternalInput").ap()

    def dscr(name, shape, dt):
        kind = "ExternalOutput" if name in DEBUG_OUT else "Internal"
        Dr[name] = nc.dram_tensor(name, list(shape), dt, kind=kind).ap()

    if "A" in PHASES or "D" in PHASES:
        din("x", [T, D_MODEL])
        for p in ("ffn1", "ffn2"):
            din(p + "_w1", [D_MODEL, D_FF]); din(p + "_w3", [D_MODEL, D_FF]); din(p + "_w2", [D_FF, D_MODEL])
        din("w_in_fm", [D_MODEL, 3200]); din("w_in_v", [D_MODEL, 512]); din("w_out", [D_MODEL, D_MODEL])
    din("gains_l", [128, 64]); din("ident", [128, 128])
    extra_inputs(din)
    Dr["out"] = nc.dram_tensor("out", [T, D_MODEL], F32, kind="ExternalOutput").ap()
    dscr("h1T", [D_MODEL, T], F32)
    if "A" in PHASES:
        dscr("projT", [3072, T], BF16)
        dscr("gT", [128, T], F32)
        dscr("vtm", [T, 512], BF16)
    else:
        din("projT", [3072, T], BF16); din("gT", [128, T]); din("vtm", [T, 512], BF16)
    if "A" in PHASES and "NSA" not in PHASES and "mixT_in" in DEBUG_OUT:
        Dr["mixT"] = nc.dram_tensor("mixT", [D_MODEL, T], BF16, kind="ExternalInput").ap()
    else:
        dscr("mixT", [D_MODEL, T], BF16)
    extra_scratch(dscr)
    for p in ("c1", "c2"):
        dscr(p + "_w1", [11, 128, KT, 512], BF16); dscr(p + "_w3", [11, 128, KT, 512], BF16)
        dscr(p + "_w2", [KT, 128, FT, 128], BF16)
    dscr("c_win", [6, 128, KT, 512], BF16); dscr("c_wing", [128, KT, 128], BF16); dscr("c_winv", [128, KT, 512], BF16)
    dscr("c_wout", [4, 128, KT, 512], BF16)

    with ExitStack() as es:
        c = Ctx(nc, es, needed)
        G = types.SimpleNamespace()
        G.psum = [(es.enter_context(nc.psum_tensor(f"ps{i}", [128, 512], F32)), Buf()) for i in range(8)]
        G.ones_bf = sbt(nc, es, "ones_bf", [128, 128], BF16)
        G.ident32 = sbt(nc, es, "ident32", [128, 128], F32)
        G.gains = sbt(nc, es, "gains", [128, 64], F32)
        G.const_b = Buf()
        c.op("dve", lambda e: e.memset(G.ones_bf[:, :], 1.0), writes=[G.const_b])
        c.dma("sp", G.ident32[:, :], Dr["ident"][:, :], writes=[G.const_b])
        c.dma("sp", G.gains[:, :], Dr["gains_l"][:, :], writes=[G.const_b])
        if "A" in PHASES:
            phase_A(c, nc, G, Dr)
        if "NSA" in PHASES:
            phase_NSA(c, nc, G, Dr)
        if "S5" in PHASES:
            phase_S5(c, nc, G, Dr)
        if "D" in PHASES:
            phase_D(c, nc, G, Dr)
        c.final_wait("sp")
        if dry:
            return set(c.rec)
    return nc


def host_inputs(inp, b):
    sq = lambda a: np.ascontiguousarray(np.asarray(a)[0])
    w_in = sq(inp["w_in"])
    q = w_in[:, 0:1024]; kc = w_in[:, 1024:1280]; vc = w_in[:, 1280:1536]; ks = w_in[:, 1536:1792]
    vs = w_in[:, 1792:2048]; kw = w_in[:, 2048:2304]; vw = w_in[:, 2304:2560]; gt = w_in[:, 2560:2584]
    u = w_in[:, 2584:3608]
    gpad = np.zeros((D_MODEL, 128), np.float32); gpad[:, :24] = gt
    m = {
        "x": np.ascontiguousarray(np.asarray(inp["x"])[b]),
        "w_in_fm": np.ascontiguousarray(np.concatenate([q, kc, vc, ks, kw, u, gpad], axis=1)),
        "w_in_v": np.ascontiguousarray(np.concatenate([vs, vw], axis=1)),
        "w_out": sq(inp["w_out"]),
        "ident": np.eye(128, dtype=np.float32),
    }
    for p in ("ffn1", "ffn2"):
        for w in ("w1", "w3", "w2"):
            m[f"{p}_{w}"] = sq(inp[f"{p}_{w}"])
    gl = np.zeros((128, 64), np.float32)
    for gi, g in enumerate((sq(inp["ffn1_norm"]), sq(inp["mix_norm"]), sq(inp["ffn2_norm"]), np.asarray(inp["final_norm"]))):
        gl[:, gi * 16:(gi + 1) * 16] = g.reshape(16, 128).T
    m["gains_l"] = gl
    extra_host(m, inp)
    return m


_NC_CACHE = {}


def kernel(**inputs):
    global LAST_RES
    key = (tuple(DEBUG_OUT), tuple(PHASES))
    if key not in _NC_CACHE:
        _NC_CACHE[key] = build_nc(needed=build_nc(dry=True))
    nc = _NC_CACHE[key]
    shared = host_inputs(inputs, 0)
    in_maps = []
    for b in range(8):
        m = dict(shared)
        m["x"] = np.ascontiguousarray(np.asarray(inputs["x"])[b])
        if "mixT_in" in DEBUG_OUT:
            import ml_dtypes
            m["mixT"] = np.zeros((D_MODEL, T), ml_dtypes.bfloat16)
        in_maps.append(m)
    res = run_bass_kernel_spmd(nc, in_maps, core_ids=list(range(8)))
    LAST_RES = res
    return np.stack([np.asarray(r["out"]) for r in res.results], axis=0).astype(np.float32)
```
